# Optimizing a Trainium2 kernel written in Bass

```python
import jax
import jax.numpy as jnp
from jax import lax
import numpy as np

D_MODEL = 2048
BATCH = 4
SEQ = 4096
DEPTH = 2

GRID_W = 64
CTX_LEN = 256
D_MIX = D_MODEL
D_SSM = D_MIX // 2
D_ATTN = D_MIX - D_SSM
SSM_HEAD_DIM = 64
SSM_HEADS = D_SSM // SSM_HEAD_DIM
SSM_STATE = 128
SSM_GROUPS = 2
SSM_BC = SSM_GROUPS * SSM_STATE
D_XBC = D_SSM + 2 * SSM_BC
SSM_CONV = 5
SSM_CHUNK = 128
ATTN_HEAD_DIM = 128
ATTN_HEADS = D_ATTN // ATTN_HEAD_DIM
ATTN_KV_HEADS = 2
KV_DIM = ATTN_KV_HEADS * ATTN_HEAD_DIM
WINDOW = 128
ATTN_BLOCK = 128
ROPE_BASE = 10000.0
D_FF = 4 * D_MODEL
N_MOD = 6
D_IN_PROJ = D_XBC + D_SSM + 2 * SSM_HEADS + D_ATTN + 2 * KV_DIM
EPS = 1e-6
DT_MIN = 0.001
DT_MAX = 0.1

kernel_name = 'hybrid_ssd_swa_dit_block'


def rms_norm(x, g):
    xf = x.astype(jnp.float32)
    y = xf * lax.rsqrt(jnp.mean(xf * xf, axis=-1, keepdims=True) + EPS)
    return (y * g.astype(jnp.float32)).astype(x.dtype)


def modulation(cond, w_mod, b_mod):
    m = jax.nn.silu(cond) @ w_mod + b_mod
    return jnp.split(m, N_MOD, axis=-1)


def split_projection(p):
    c1 = D_XBC
    c2 = c1 + D_SSM
    c3 = c2 + 2 * SSM_HEADS
    c4 = c3 + D_ATTN
    c5 = c4 + KV_DIM
    return jnp.split(p, [c1, c2, c3, c4, c5], axis=-1)


def centred_depthwise_conv(u, w, bias):
    y = lax.conv_general_dilated(
        u, w[:, None, :].astype(u.dtype), window_strides=(1,),
        padding=[(SSM_CONV // 2, SSM_CONV // 2)],
        dimension_numbers=('NWC', 'WIO', 'NWC'),
        feature_group_count=u.shape[-1])
    return y + bias


def ssd_chunked(xh, dt, a, bm, cm, h0):
    dtype = xh.dtype
    f32 = jnp.float32
    b, L, H, P = xh.shape
    G, N, Q = SSM_GROUPS, SSM_STATE, SSM_CHUNK
    R = H // G
    nc = L // Q
    log_dec = (dt.astype(f32) * a.astype(f32)).reshape(b, nc, Q, G, R)
    a_cs = jnp.cumsum(log_dec, axis=2)
    xd = (xh.astype(f32) * dt.astype(f32)[..., None]).reshape(b, nc, Q, G, R, P)
    bc = bm.astype(f32).reshape(b, nc, Q, G, N)
    cc = cm.astype(f32).reshape(b, nc, Q, G, N)
    seg = a_cs[:, :, :, None] - a_cs[:, :, None, :]
    lower = jnp.tril(jnp.ones((Q, Q), dtype=bool))[None, None, :, :, None, None]
    decay = jnp.exp(jnp.where(lower, seg, -jnp.inf))
    cb = jnp.einsum('bclgn,bcsgn->bclsg', cc, bc)
    y_diag = jnp.einsum('bclsgr,bcsgrp->bclgrp', cb[..., None] * decay, xd)
    decay_to_end = jnp.exp(a_cs[:, :, -1:] - a_cs)
    states = jnp.einsum('bcsgn,bcsgrp->bcgrpn', bc, xd * decay_to_end[..., None])
    chunk_decay = jnp.exp(a_cs[:, :, -1])

    def step(h, inp):
        st, dec = inp
        return h * dec[..., None, None] + st, h

    h_final, h_in = lax.scan(
        step, h0.astype(f32).reshape(b, G, R, P, N),
        (jnp.moveaxis(states, 1, 0), jnp.moveaxis(chunk_decay, 1, 0)))
    h_in = jnp.moveaxis(h_in, 0, 1)
    y_off = jnp.einsum('bclgn,bcgrpn->bclgrp', cc, h_in) * jnp.exp(a_cs)[..., None]
    y = (y_diag + y_off).reshape(b, L, H, P)
    return y.astype(dtype), h_final.reshape(b, H, P, N)


def bidirectional_ssd(xbc, dt_raw, z, a_log, dt_bias, d_skip, norm_w, h0_fwd, h0_bwd):
    b, L, _ = xbc.shape
    xs, bm, cm = jnp.split(xbc, [D_SSM, D_SSM + SSM_BC], axis=-1)
    xh = xs.reshape(b, L, SSM_HEADS, SSM_HEAD_DIM)
    bm = bm.reshape(b, L, SSM_GROUPS, SSM_STATE)
    cm = cm.reshape(b, L, SSM_GROUPS, SSM_STATE)
    a = -jnp.exp(a_log.astype(jnp.float32))
    dt = jax.nn.softplus(dt_raw.astype(jnp.float32).reshape(b, L, 2, SSM_HEADS)
                         + dt_bias.astype(jnp.float32))
    flip = lambda t: jnp.flip(t, axis=1)
    y_f, h_f = ssd_chunked(xh, dt[:, :, 0], a[0], bm, cm, h0_fwd)
    y_b, h_b = ssd_chunked(flip(xh), flip(dt[:, :, 1]), a[1], flip(bm), flip(cm), h0_bwd)
    y = y_f + flip(y_b) + xh * d_skip[:, None].astype(xh.dtype)
    y = y.reshape(b, L, D_SSM) * jax.nn.silu(z)
    return rms_norm(y, norm_w), h_f, h_b


def axial_rope(x, pos_row, pos_col):
    n_freq = ATTN_HEAD_DIM // 4
    inv_freq = ROPE_BASE ** (-jnp.arange(n_freq, dtype=jnp.float32) / n_freq)

    def rot(u, pos):
        ang = pos[:, None] * inv_freq[None, :]
        cos = jnp.cos(ang)[None, :, None, :]
        sin = jnp.sin(ang)[None, :, None, :]
        u1, u2 = jnp.split(u, 2, axis=-1)
        return jnp.concatenate([u1 * cos - u2 * sin, u2 * cos + u1 * sin], axis=-1)

    xr, xcol = jnp.split(x.astype(jnp.float32), 2, axis=-1)
    return jnp.concatenate([rot(xr, pos_row), rot(xcol, pos_col)], axis=-1).astype(x.dtype)


def windowed_attention(q, k, v, kc, vc, sink):
    b, L = q.shape[:2]
    T = ATTN_BLOCK
    nb = L // T
    R = ATTN_HEADS // ATTN_KV_HEADS
    scale = ATTN_HEAD_DIM ** -0.5
    qb = q.reshape(b, nb, T, ATTN_KV_HEADS, R, ATTN_HEAD_DIM)
    pad = ((0, 0), (T, T), (0, 0), (0, 0))
    kp = jnp.pad(k, pad).reshape(b, nb + 2, T, ATTN_KV_HEADS, ATTN_HEAD_DIM)
    vp = jnp.pad(v, pad).reshape(b, nb + 2, T, ATTN_KV_HEADS, ATTN_HEAD_DIM)
    kband = jnp.concatenate([kp[:, :-2], kp[:, 1:-1], kp[:, 2:]], axis=2)
    vband = jnp.concatenate([vp[:, :-2], vp[:, 1:-1], vp[:, 2:]], axis=2)
    s_loc = jnp.einsum('bnqhrd,bnkhd->bnhrqk', qb, kband).astype(jnp.float32) * scale
    blk = jnp.arange(nb)[:, None, None]
    qpos = blk * T + jnp.arange(T)[None, :, None]
    kpos = (blk - 1) * T + jnp.arange(3 * T)[None, None, :]
    valid = (jnp.abs(qpos - kpos) <= WINDOW) & (kpos >= 0) & (kpos < L)
    s_loc = jnp.where(valid[None, :, None, None], s_loc, -jnp.inf)
    s_ctx = jnp.einsum('bnqhrd,bkhd->bnhrqk', qb, kc).astype(jnp.float32) * scale
    s_sink = jnp.broadcast_to(
        sink.astype(jnp.float32).reshape(ATTN_KV_HEADS, R)[None, None, :, :, None, None],
        s_loc.shape[:-1] + (1,))
    p = jax.nn.softmax(jnp.concatenate([s_loc, s_ctx, s_sink], axis=-1), axis=-1)
    p = p.astype(v.dtype)
    n_loc = 3 * T
    n_ctx = kc.shape[1]
    out = (jnp.einsum('bnhrqk,bnkhd->bnqhrd', p[..., :n_loc], vband)
           + jnp.einsum('bnhrqk,bkhd->bnqhrd', p[..., n_loc:n_loc + n_ctx], vc))
    return out.reshape(b, L, ATTN_HEADS * ATTN_HEAD_DIM)


def context_attention(qc, kc, vc, sink):
    b, Lc = qc.shape[:2]
    R = ATTN_HEADS // ATTN_KV_HEADS
    qg = qc.reshape(b, Lc, ATTN_KV_HEADS, R, ATTN_HEAD_DIM)
    s = jnp.einsum('bqhrd,bkhd->bhrqk', qg, kc).astype(jnp.float32) * ATTN_HEAD_DIM ** -0.5
    s_sink = jnp.broadcast_to(
        sink.astype(jnp.float32).reshape(ATTN_KV_HEADS, R)[None, :, :, None, None],
        s.shape[:-1] + (1,))
    p = jax.nn.softmax(jnp.concatenate([s, s_sink], axis=-1), axis=-1).astype(vc.dtype)
    out = jnp.einsum('bhrqk,bkhd->bqhrd', p[..., :Lc], vc)
    return out.reshape(b, Lc, ATTN_HEADS * ATTN_HEAD_DIM)


def hybrid_mixer(h, hc, w_in, conv_w, conv_b, a_log, dt_bias, d_skip, ssm_norm,
                 attn_sink, w_out, ctx_out):
    b, L, _ = h.shape
    Lc = hc.shape[1]
    xbc, z, dt_raw, q, k, v = split_projection(h @ w_in)
    xbc_c, z_c, dt_c, q_c, k_c, v_c = split_projection(hc @ w_in)
    xbc = jax.nn.silu(centred_depthwise_conv(xbc, conv_w, conv_b))
    xbc_c = jax.nn.silu(centred_depthwise_conv(xbc_c, conv_w, conv_b))
    h0 = jnp.zeros((b, SSM_HEADS, SSM_HEAD_DIM, SSM_STATE), jnp.float32)
    y_ssm_c, h_fwd, h_bwd = bidirectional_ssd(xbc_c, dt_c, z_c, a_log, dt_bias, d_skip,
                                              ssm_norm, h0, h0)
    y_ssm, _, _ = bidirectional_ssd(xbc, dt_raw, z, a_log, dt_bias, d_skip, ssm_norm,
                                    h_fwd, h_bwd)
    rows = L // GRID_W
    pos_row = jnp.broadcast_to(jnp.arange(rows, dtype=jnp.float32)[:, None],
                               (rows, GRID_W)).reshape(-1)
    pos_col = jnp.broadcast_to(jnp.arange(GRID_W, dtype=jnp.float32)[None, :],
                               (rows, GRID_W)).reshape(-1)
    q = axial_rope(q.reshape(b, L, ATTN_HEADS, ATTN_HEAD_DIM), pos_row, pos_col)
    k = axial_rope(k.reshape(b, L, ATTN_KV_HEADS, ATTN_HEAD_DIM), pos_row, pos_col)
    v = v.reshape(b, L, ATTN_KV_HEADS, ATTN_HEAD_DIM)
    kc = k_c.reshape(b, Lc, ATTN_KV_HEADS, ATTN_HEAD_DIM)
    vc = v_c.reshape(b, Lc, ATTN_KV_HEADS, ATTN_HEAD_DIM)
    y_attn = windowed_attention(q, k, v, kc, vc, attn_sink)
    out = jnp.concatenate([y_ssm, y_attn], axis=-1) @ w_out
    if ctx_out:
        y_attn_c = context_attention(q_c.reshape(b, Lc, ATTN_HEADS, ATTN_HEAD_DIM), kc, vc,
                                     attn_sink)
        out_c = jnp.concatenate([y_ssm_c, y_attn_c], axis=-1) @ w_out
    else:
        out_c = None
    return out, out_c


def squared_relu_mlp(h, w1, w2):
    return jnp.square(jax.nn.relu(h @ w1)) @ w2


def setup_inputs(seed: int = 0) -> dict:
    key = jax.random.key(seed)
    ks = jax.random.split(key, 24)
    f32 = jnp.float32
    nrm = lambda k, shape, s: jax.random.normal(k, shape, f32) * s
    dt0 = jnp.exp(jax.random.uniform(ks[13], (DEPTH, 2, SSM_HEADS), f32,
                                     np.log(DT_MIN), np.log(DT_MAX)))
    return {
        'x': nrm(ks[0], (BATCH, SEQ, D_MODEL), 1.0),
        'c': nrm(ks[1], (BATCH, D_MODEL), 1.0),
        'ctx': nrm(ks[2], (BATCH, CTX_LEN, D_MODEL), 1.0),
        'c_ctx': nrm(ks[3], (D_MODEL,), 1.0),
        'w_mod': nrm(ks[4], (DEPTH, D_MODEL, N_MOD * D_MODEL), 0.5 * D_MODEL ** -0.5),
        'b_mod': nrm(ks[5], (DEPTH, N_MOD * D_MODEL), 0.02),
        'g_pre_mix': 1.0 + nrm(ks[6], (DEPTH, D_MODEL), 0.05),
        'g_post_mix': 1.0 + nrm(ks[7], (DEPTH, D_MODEL), 0.05),
        'g_pre_mlp': 1.0 + nrm(ks[8], (DEPTH, D_MODEL), 0.05),
        'g_post_mlp': 1.0 + nrm(ks[9], (DEPTH, D_MODEL), 0.05),
        'w_in': nrm(ks[10], (DEPTH, D_MODEL, D_IN_PROJ), D_MODEL ** -0.5),
        'conv_w': nrm(ks[11], (DEPTH, SSM_CONV, D_XBC), SSM_CONV ** -0.5),
        'conv_b': nrm(ks[12], (DEPTH, D_XBC), 0.02),
        'a_log': jnp.log(jax.random.uniform(ks[14], (DEPTH, 2, SSM_HEADS), f32, 1.0, 16.0)),
        'dt_bias': dt0 + jnp.log(-jnp.expm1(-dt0)),
        'd_skip': 1.0 + nrm(ks[15], (DEPTH, SSM_HEADS), 0.1),
        'ssm_norm': 1.0 + nrm(ks[16], (DEPTH, D_SSM), 0.05),
        'attn_sink': nrm(ks[17], (DEPTH, ATTN_HEADS), 0.5),
        'w_out': nrm(ks[18], (DEPTH, D_MIX, D_MODEL), D_MIX ** -0.5),
        'w_ff1': nrm(ks[19], (DEPTH, D_MODEL, D_FF), D_MODEL ** -0.5),
        'w_ff2': nrm(ks[20], (DEPTH, D_FF, D_MODEL), D_FF ** -0.5),
    }


def reference(x, c, ctx, c_ctx, w_mod, b_mod, g_pre_mix, g_post_mix, g_pre_mlp,
              g_post_mlp, w_in, conv_w, conv_b, a_log, dt_bias, d_skip, ssm_norm,
              attn_sink, w_out, w_ff1, w_ff2):
    xc = ctx
    for i in range(DEPTH):
        last = i == DEPTH - 1
        sh_m, sc_m, g_m, sh_f, sc_f, g_f = [m[:, None, :] for m in
                                            modulation(c, w_mod[i], b_mod[i])]
        csh_m, csc_m, cg_m, csh_f, csc_f, cg_f = modulation(c_ctx, w_mod[i], b_mod[i])
        h = rms_norm(x, g_pre_mix[i]) * (1.0 + sc_m) + sh_m
        hc = rms_norm(xc, g_pre_mix[i]) * (1.0 + csc_m) + csh_m
        mix, mix_c = hybrid_mixer(h, hc, w_in[i], conv_w[i], conv_b[i], a_log[i],
                                  dt_bias[i], d_skip[i], ssm_norm[i], attn_sink[i],
                                  w_out[i], not last)
        x = x + g_m * rms_norm(mix, g_post_mix[i])
        f = squared_relu_mlp(rms_norm(x, g_pre_mlp[i]) * (1.0 + sc_f) + sh_f,
                             w_ff1[i], w_ff2[i])
        x = x + g_f * rms_norm(f, g_post_mlp[i])
        if not last:
            xc = xc + cg_m * rms_norm(mix_c, g_post_mix[i])
            fc = squared_relu_mlp(rms_norm(xc, g_pre_mlp[i]) * (1.0 + csc_f) + csh_f,
                                  w_ff1[i], w_ff2[i])
            xc = xc + cg_f * rms_norm(fc, g_post_mlp[i])
    return x
```

```python
import numpy as np
import concourse.bass as bass
import concourse.mybir as mybir
from concourse.bass_utils import run_bass_kernel_spmd

F32 = mybir.dt.float32
BF16 = mybir.dt.bfloat16
AF = mybir.ActivationFunctionType
ALU = mybir.AluOpType

D = 2048
KC = 16
LC = 256
D_SSM = 1024
D_XBC = 1536
NH = 16
HP = 64
NST = 128
D_FF = 8192
EPS = 1e-6
BIG = 30000.0
NBLK_L = 69
OFF_WMOD, OFF_WINF, OFF_WINT, OFF_WOUT, OFF_FF1, OFF_FF2 = 0, 24, 30, 33, 37, 53
NVF = 232
NVB = 1112
ENG = ['pe', 'act', 'dve', 'pool', 'sp']


class Sched:
    def __init__(self, nc):
        self.nc = nc
        self.ops = {e: [] for e in ENG}
        self.lastw = {}
        self.readers = {}
        self.seen = {e: {} for e in ENG}
        self.seend = {e: {} for e in ENG}
        self.chan_cnt = {}

    @staticmethod
    def _norm(b):
        return b if isinstance(b, tuple) else (b, None)

    def _deps(self, reads, writes):
        deps = []
        for (b, k) in reads:
            lw = self.lastw.get(b, {})
            toks = list(lw.values()) if k is None else [lw[x] for x in (k, None) if x in lw]
            deps += [(t, 'raw') for t in toks]
        for (b, k) in writes:
            lw = self.lastw.get(b, {})
            rd = self.readers.get(b, {})
            if k is None:
                deps += [(t, 'waw') for t in lw.values()]
                for d_ in rd.values():
                    deps += [(t, 'war') for t in d_.values()]
            else:
                for x in (k, None):
                    if x in lw:
                        deps.append((lw[x], 'waw'))
                    if x in rd:
                        deps += [(t, 'war') for t in rd[x].values()]
        return deps

    def _commit(self, tok, reads, writes):
        for (b, k) in reads:
            d_ = self.readers.setdefault(b, {}).setdefault(k, {})
            d_[(tok[0], tok[1])] = tok
        for (b, k) in writes:
            if k is None:
                self.lastw[b] = {None: tok}
                self.readers[b] = {}
            else:
                self.lastw.setdefault(b, {})[k] = tok
                self.readers.setdefault(b, {})[k] = {}

    def op(self, eng, fn, reads=(), writes=(), chan=None):
        reads = [self._norm(r) for r in reads]
        writes = [self._norm(w) for w in writes]
        deps = self._deps(reads, writes)
        idx = len(self.ops[eng])
        waits = []
        for (tok, kind) in deps:
            if tok[0] == 'e':
                _, A, i = tok
                if A == eng and (eng == 'pe' or kind != 'raw'):
                    continue
                if self.seen[eng].get(A, -1) >= i:
                    continue
                self.seen[eng][A] = i
                waits.append(tok)
                self.ops[A][i]['sig'] = True
            else:
                _, ch, cnt = tok
                cnt = self.chan_cnt[ch]
                if self.seend[eng].get(ch, 0) >= cnt:
                    continue
                self.seend[eng][ch] = cnt
                waits.append(('d', ch, cnt))
        self.ops[eng].append(dict(fn=fn, waits=waits, sig=False, chan=chan))
        if chan is None:
            tok = ('e', eng, idx)
        else:
            self.chan_cnt[chan] = self.chan_cnt.get(chan, 0) + 16
            tok = ('d', chan, self.chan_cnt[chan])
        self._commit(tok, reads, writes)

    def barrier(self):
        for e in ENG:
            for A in ENG:
                if A == e or not self.ops[A]:
                    continue
                i = len(self.ops[A]) - 1
                while i >= 0 and self.ops[A][i]['chan'] is not None:
                    i -= 1
                if i < 0 or self.seen[e].get(A, -1) >= i:
                    continue
                self.seen[e][A] = i
                self.ops[A][i]['sig'] = True
                self._pending_bar.setdefault(e, []).append(('e', A, i))
            for ch, cnt in self.chan_cnt.items():
                if self.seend[e].get(ch, 0) >= cnt:
                    continue
                self.seend[e][ch] = cnt
                self._pending_bar.setdefault(e, []).append(('d', ch, cnt))

    _pending_bar = None

    def emit(self):
        nc = self.nc
        sems = {e: nc.alloc_semaphore(f"s_{e}") for e in ENG}
        chsem = {ch: nc.alloc_semaphore(f"c_{ch}") for ch in self.chan_cnt}
        for e in ENG:
            c = 0
            for o in self.ops[e]:
                if o['chan'] is None and o['sig']:
                    c += 1
                    o['sv'] = c
        ops = self.ops
        chan_cnt = self.chan_cnt

        def run(e, eng):
            for o in ops[e]:
                best = {}
                for w in o['waits']:
                    if w[0] == 'e':
                        key = ('e', w[1])
                        val = ops[w[1]][w[2]]['sv']
                    else:
                        key = ('d', w[1])
                        val = w[2]
                    best[key] = max(best.get(key, 0), val)
                for key, val in best.items():
                    sem = sems[key[1]] if key[0] == 'e' else chsem[key[1]]
                    eng.wait_ge(sem, val)
                ins = o['fn'](eng)
                if o['chan'] is not None:
                    ins.then_inc(chsem[o['chan']], 16)
                elif o['sig']:
                    ins.then_inc(sems[e], 1)
            if e == 'sp':
                for ch, cnt in chan_cnt.items():
                    eng.wait_ge(chsem[ch], cnt)

        with nc.Block() as block:
            block.tensor(lambda t: run('pe', t))
            block.scalar(lambda t: run('act', t))
            block.vector(lambda t: run('dve', t))
            block.gpsimd(lambda t: run('pool', t))
            block.sync(lambda t: run('sp', t))


class Prog:
    def __init__(self, L, NL, dbg=()):
        self.L = L
        self.NL = NL
        self.NT = LC + L
        self.dbg = set(dbg)
        self.nc = bass.Bass("TRN2", target_bir_lowering=False)
        self.S = Sched(self.nc)
        self.S._pending_bar = {}
        nc = self.nc
        self.arena = nc.alloc_sbuf_tensor("arena", [128, 53000], F32)
        self.aoff = 0
        self.psum = nc.alloc_psum_tensor("psum", [128, 4096], F32)
        self.ps_i = 0
        self.ring_i = 0
        self.NS = 3

    def alloc(self, n_f32):
        o = self.aoff
        self.aoff += (n_f32 + 7) // 8 * 8
        assert self.aoff <= 53000, self.aoff
        return self.arena[:, o:o + n_f32]

    def alloc_bf(self, n_bf16):
        return self.alloc((n_bf16 + 1) // 2).bitcast(BF16)

    def ps(self):
        b = self.ps_i % 8
        self.ps_i += 1
        return self.psum[:, b * 512:(b + 1) * 512], ('ps', b)

    def bank(self, b):
        return self.psum[:, b * 512:(b + 1) * 512], ('ps', b)

    def bar(self):
        S = self.S
        S._pending_bar = {}
        S.barrier()
        pend = S._pending_bar
        for e, lst in pend.items():
            self._prewaits.setdefault(e, []).extend(lst)

    _prewaits = None

    def op(self, eng, fn, reads=(), writes=(), chan=None):
        self.S.op(eng, fn, reads, writes, chan)
        if self._prewaits and eng in self._prewaits:
            self.S.ops[eng][-1]['waits'].extend(self._prewaits.pop(eng))

    def mm(self, out, lhsT, rhs, start, stop, reads, writes):
        self.op('pe', lambda e: e.matmul(out, lhsT, rhs, start=start, stop=stop), reads, writes)

    def tr(self, out, in_, ident, reads, writes):
        self.op('pe', lambda e: e.transpose(out, in_, ident), reads, writes)

    def act(self, out, in_, func, reads, writes, bias=None, scale=None, accum_out=None, eng='act'):
        kw = {}
        if bias is not None:
            kw['bias'] = bias
        if scale is not None:
            kw['scale'] = scale
        if accum_out is not None:
            kw['accum_out'] = accum_out
        self.op(eng, lambda e: e.activation(out=out, in_=in_, func=func, **kw), reads, writes)

    def tt(self, eng, out, in0, in1, op, reads, writes):
        self.op(eng, lambda e: e.tensor_tensor(out=out, in0=in0, in1=in1, op=op), reads, writes)

    def ts(self, eng, out, in0, s1, s2, op0, op1, reads, writes):
        if s2 is None:
            self.op(eng, lambda e: e.tensor_scalar(out=out, in0=in0, scalar1=s1, scalar2=None, op0=op0),
                    reads, writes)
        else:
            self.op(eng, lambda e: e.tensor_scalar(out=out, in0=in0, scalar1=s1, scalar2=s2, op0=op0, op1=op1),
                    reads, writes)

    def stt(self, eng, out, in0, scalar, in1, op0, op1, reads, writes):
        self.op(eng, lambda e: e.scalar_tensor_tensor(out=out, in0=in0, scalar=scalar, in1=in1, op0=op0, op1=op1),
                reads, writes)

    def cp(self, eng, out, in_, reads, writes):
        if eng == 'act':
            self.op(eng, lambda e: e.copy(out=out, in_=in_), reads, writes)
        else:
            self.op(eng, lambda e: e.tensor_copy(out=out, in_=in_), reads, writes)

    def recip(self, out, in_, reads, writes):
        self.op('dve', lambda e: e.reciprocal(out=out, in_=in_), reads, writes)

    def memset(self, eng, ap, val, writes):
        self.op(eng, lambda e: e.memset(ap, val), (), writes)

    def dma(self, eng, out, in_, reads, writes, chan):
        self.op(eng, lambda e: e.dma_start(out=out, in_=in_), reads, writes, chan)


def blk(l, off, i):
    return l * NBLK_L + off + i


WGRP = [(0, 8, 'm0'), (8, 16, 'm1'), (16, 24, 'm2'), (24, 33, 'in'), (33, 37, 'wo'),
        (37, 45, 'f1a'), (45, 53, 'f1b'), (53, 61, 'f2a'), (61, 69, 'f2b')]


def wgrp(b):
    l, r = divmod(b, NBLK_L)
    for lo, hi, nm in WGRP:
        if lo <= r < hi:
            return f"cv{l}{nm}"
    raise ValueError


def build_program(L=4096, NL=2, dbg=(), stop_after=None):
    P = Prog(L, NL, dbg)
    P._prewaits = {}
    nc = P.nc
    NT = P.NT
    NB = NT // 128
    nlat_tiles = L // 512
    tiles = [(0, 256, True)] + [(LC + i * 512, 512, False) for i in range(nlat_tiles)]
    nlc = L // 128

    def dram(name, shape, dt, kind="Internal"):
        if name in P.dbg:
            kind = "ExternalOutput"
        return nc.dram_tensor(name, shape, dt, kind=kind).ap()

    xT0 = dram("xT0", [D, NT], F32, "ExternalInput")
    cc_d = dram("cc", [128, KC * 2], F32, "ExternalInput")
    wall = dram("wall", [NL * NBLK_L, 128, 8192], F32, "ExternalInput")
    vecF_d = dram("vecF", [NL, 128, NVF], F32, "ExternalInput")
    vecB_d = dram("vecB", [NL, 1, NVB], F32, "ExternalInput")
    cmat_d = dram("cmat", [128, 7 * 128], F32, "ExternalInput")
    amask_d = dram("amask", [128, 2 * 512], F32, "ExternalInput")
    rope_d = dram("rope", [2, 128, L], F32, "ExternalInput")
    outT = dram("outT", [D, L], F32, "ExternalOutput")

    wbf_l = [dram(f"wbf{l_}", [NBLK_L, 128, 8192], BF16) for l_ in range(NL)]

    class _WB:
        def __getitem__(self, b):
            return wbf_l[b // NBLK_L][b % NBLK_L]
    wbf = _WB()
    xTs = dram("xTs", [D, NT], F32)
    xbcT = dram("xbcT", [D_XBC, NT], F32)
    zs_d = dram("zs", [NT, 1024], F32)
    dts_d = dram("dts", [NT, 32], F32)
    qT_d = dram("qT", [1024, NT], BF16)
    kT_d = dram("kT", [256, NT], BF16)
    vtok_d = dram("vtok", [NT, 256], BF16)
    xcT_d = dram("xcT", [D_XBC, NT], BF16)
    xtok_d = dram("xtok", [NT, 1280], BF16)
    yf_d = dram("yf", [NT, 1024], F32)
    yT_d = dram("yT", [D, NT], BF16)

    ring = [P.alloc_bf(8192) for _ in range(P.NS)]
    cmat = P.alloc(7 * 128)
    ident_f = cmat[:, 0:128]
    tri = [cmat[:, 128:256], cmat[:, 256:384]]
    mbias = [cmat[:, 384:512], cmat[:, 512:640]]
    rot_f = cmat[:, 640:768]
    cbf = P.alloc_bf(2 * 128)
    ident_b = cbf[:, 0:128]
    ones_b = cbf[:, 128:256]
    amask = P.alloc_bf(1024)
    cc = P.alloc(32)
    scb = P.alloc_bf(32)
    vecF = P.alloc(NVF)
    vecB = P.alloc(NVB)
    mod = P.alloc(192)
    der = P.alloc(4 * 32)
    a_bc = P.alloc(32)
    esink = P.alloc(8)
    expsink = P.alloc(1024)
    persist_end = P.aoff
    amask_f = P.alloc(1024)

    mod3 = mod.rearrange("p (c j) -> p c j", j=2)
    A_m = der[:, 0:32].rearrange("p (c j) -> p c j", j=2)
    G_m = der[:, 32:64].rearrange("p (c j) -> p c j", j=2)
    A_f = der[:, 64:96].rearrange("p (c j) -> p c j", j=2)
    G_f = der[:, 96:128].rearrange("p (c j) -> p c j", j=2)
    B_m = mod3[:, 0:16, :]
    B_f = mod3[:, 48:64, :]

    for b in range(NL * NBLK_L):
        P.dma('pool', wbf[b], wall[b], reads=[], writes=[('wbf', wgrp(b))], chan=wgrp(b))

    P.dma('sp', cmat, cmat_d, [], ['cmat'], 'ld_c')
    P.dma('sp', amask_f, amask_d, [], ['amask_f'], 'ld_c')
    P.dma('sp', cc, cc_d, [], ['cc'], 'ld_c')
    P.cp('dve', ident_b, ident_f, ['cmat'], ['cbf'])
    P.cp('dve', ones_b, cmat[:, 768:896], ['cmat'], ['cbf'])
    P.cp('dve', amask, amask_f, ['amask_f'], ['amask'])

    class Prefetch:
        def __init__(self, seq, depth=2):
            self.seq = seq
            self.i = 0
            self.issued = 0
            self.depth = depth
            self.slots = {}

        def _issue(self):
            b = self.seq[self.issued]
            slot = P.ring_i % P.NS
            P.ring_i += 1
            P.dma('sp', ring[slot], wbf[b], reads=[('wbf', wgrp(b))], writes=[('ring', slot)], chan=f"ring{slot}")
            self.slots[self.issued] = slot
            self.issued += 1

        def get(self):
            while self.issued < len(self.seq) and self.issued <= self.i + self.depth:
                self._issue()
            slot = self.slots.pop(self.i)
            self.i += 1
            return ring[slot], ('ring', slot)

    def layer_setup(l):
        P.dma('sp', vecF, vecF_d[l], [], ['vecF'], 'ld_c')
        P.dma('sp', vecB, vecB_d[l].partition_broadcast(128), [], ['vecB'], 'ld_c')
        P.act(a_bc, vecB[:, 0:32], AF.Exp, ['vecB'], ['a_bc'])
        P.ts('dve', a_bc, a_bc, -1.0, None, ALU.mult, None, ['a_bc'], ['a_bc'])
        P.act(esink, vecB[:, 1104:1112], AF.Exp, ['vecB'], ['esink'])
        P.cp('dve', expsink.rearrange("p (h t) -> p h t", t=128),
             esink.unsqueeze(2).to_broadcast([128, 8, 128]), ['esink'], ['expsink'])

    def phase_mod(l):
        P.act(scb, cc, AF.Silu, ['cc'], ['scb'])
        scb3 = scb.rearrange("p (c j) -> p c j", j=2)
        pf = Prefetch([blk(l, OFF_WMOD, i) for i in range(24)])
        psM, kM = P.ps()
        for og in range(24):
            w, kw = pf.get()
            w3 = w.rearrange("p (k o) -> p k o", o=512)
            for oc in range(4):
                col = (og * 4 + oc) * 2
                for kc in range(KC):
                    P.mm(psM[:, col:col + 2], w3[:, kc, oc * 128:(oc + 1) * 128], scb3[:, kc, :],
                         kc == 0, kc == KC - 1, [kw, 'scb'], [kM])
        bmod = vecF[:, 136:232]
        P.tt('dve', mod3, psM[:, 0:192].rearrange("p (c j) -> p c j", j=2),
             bmod.unsqueeze(2).to_broadcast([128, 96, 2]), ALU.add, [kM, 'vecF'], ['mod'])

        def gb(i):
            return vecF[:, i * 16:(i + 1) * 16].unsqueeze(2).to_broadcast([128, 16, 2])
        P.stt('dve', A_m, mod3[:, 16:32, :], 1.0, gb(0), ALU.add, ALU.mult, ['mod', 'vecF'], ['der'])
        P.tt('dve', G_m, mod3[:, 32:48, :], gb(1), ALU.mult, ['mod', 'vecF'], ['der'])
        P.stt('dve', A_f, mod3[:, 64:80, :], 1.0, gb(2), ALU.add, ALU.mult, ['mod', 'vecF'], ['der'])
        P.tt('dve', G_f, mod3[:, 80:96, :], gb(3), ALU.mult, ['mod', 'vecF'], ['der'])

    def rms_rstd(x3, kx, sq3, ksq, rstd, krstd, T, nch=KC, dim=D):
        P.tt('pool', sq3[:, :, :T], x3, x3, ALU.mult, [kx], [ksq])
        pss, kps = P.ps()
        for c in range(nch):
            P.mm(pss[:, :T], ones_b, sq3[:, c, :T], c == 0, c == nch - 1, [ksq, 'cbf'], [kps])
        P.act(rstd[:, :T], pss[:, :T], AF.Sqrt, [kps], [krstd], bias=EPS, scale=1.0 / dim)
        P.recip(rstd[:, :T], rstd[:, :T], [krstd], [krstd])

    def phase_inproj(l):
        P.bar()
        P.aoff = persist_end
        xsb = [P.alloc(8192)]
        xsb.append(xsb[0])
        xbcst = P.alloc(12 * 512)
        sqz = P.alloc(4096)
        hbuf = P.alloc_bf(8192)
        rstd = P.alloc(512)
        tmp = [P.alloc(512) for _ in range(3)]
        qf = [P.alloc(512) for _ in range(2)]
        t1 = [P.alloc(512) for _ in range(2)]
        qst = P.alloc_bf(10 * 512)
        cs = [P.alloc(1024), P.alloc(1024)]
        vst = P.alloc_bf(4 * 256)
        dst = P.alloc(4 * 32)
        dtt = P.alloc(32)
        xsrc = xT0 if l == 0 else xTs
        seq = []
        for _ in tiles:
            seq += [blk(l, OFF_WINF, i) for i in range(6)] + [blk(l, OFF_WINT, i) for i in range(3)]
        pf = Prefetch(seq)
        tmp_i = [0]

        def load_x(ti):
            t0, T, isctx = tiles[ti]
            xs = xsb[ti % 2].rearrange("p (c t) -> p c t", t=512)[:, :, :T]
            P.dma('sp', xs, xsrc[:, t0:t0 + T].rearrange("(c p) t -> p c t", p=128),
                  [('xT', None)] if l > 0 else [], ['xs0'], 'ld_xs')
            if not isctx:
                P.dma('sp', cs[ti % 2].rearrange("p (a t) -> p a t", t=512)[:, :, :T],
                      rope_d[:, :, t0 - LC:t0 - LC + T].rearrange("a p t -> p a t"), [], [f'cs{ti % 2}'],
                      f'ld_cs{ti % 2}')

        for ti, (t0, T, isctx) in enumerate(tiles):
            load_x(ti)
            j = 1 if isctx else 0
            kx = 'xs0'
            xs = xsb[ti % 2].rearrange("p (c t) -> p c t", t=512)[:, :, :T]
            sq3 = sqz.bitcast(BF16).rearrange("p (c t) -> p c t", t=512)
            h3 = hbuf.rearrange("p (c t) -> p c t", t=512)
            rms_rstd(xs, kx, sq3, 'sqz', rstd, 'rstd', T)
            for c in range(KC):
                tm = tmp[tmp_i[0] % 3]
                ktm = f'tmp{tmp_i[0] % 3}'
                tmp_i[0] += 1
                P.stt('dve', tm[:, :T], xs[:, c, :], A_m[:, c, j:j + 1], rstd[:, :T], ALU.mult, ALU.mult,
                      [kx, 'der', 'rstd'], [ktm])
                P.act(h3[:, c, :T], tm[:, :T], AF.Identity, [ktm, 'mod'], [('h', c)], bias=B_m[:, c, j:j + 1],
                      scale=1.0)
            xb3 = xbcst.rearrange("p (c t) -> p c t", t=512)
            q3 = qst.rearrange("p (c t) -> p c t", t=512)
            cst = cs[ti % 2].rearrange("p (a t) -> p a t", t=512)
            for b in range(6):
                w, kw = pf.get()
                w3 = w.rearrange("p (k o) -> p k o", o=512)
                for oc in range(4):
                    g = b * 4 + oc
                    if g >= 22:
                        break
                    pso, kp = P.ps()
                    for kc in range(KC):
                        P.mm(pso[:, :T], w3[:, kc, oc * 128:(oc + 1) * 128], h3[:, kc, :T], kc == 0, kc == KC - 1,
                             [kw, ('h', kc)], [kp])
                    if g < 12:
                        if g % 2 == 0:
                            P.cp('act', xb3[:, g, :T], pso[:, :T], [kp], [('xbcst', g)])
                        else:
                            P.cp('dve', xb3[:, g, :T], pso[:, :T], [kp], [('xbcst', g)])
                    else:
                        hi = g - 12
                        if isctx:
                            P.cp('act', q3[:, hi, :T], pso[:, :T], [kp], [('qst', hi)])
                        else:
                            qq = qf[hi % 2]
                            kq = f'qf{hi % 2}'
                            tt1 = t1[hi % 2]
                            kt1 = f't1{hi % 2}'
                            P.cp('act', qq[:, :T], pso[:, :T], [kp], [kq])
                            ps2, kp2 = P.ps()
                            P.mm(ps2[:, :T], rot_f, qq[:, :T], True, True, [kq, 'cmat'], [kp2])
                            P.tt('pool', tt1[:, :T], qq[:, :T], cst[:, 0, :T], ALU.mult, [kq, f'cs{ti % 2}'], [kt1])
                            tm = tmp[tmp_i[0] % 3]
                            ktm = f'tmp{tmp_i[0] % 3}'
                            tmp_i[0] += 1
                            P.tt('dve', tm[:, :T], ps2[:, :T], cst[:, 1, :T], ALU.mult, [kp2, f'cs{ti % 2}'], [ktm])
                            P.tt('dve', q3[:, hi, :T], tm[:, :T], tt1[:, :T], ALU.add, [ktm, kt1], [('qst', hi)])
                if b == 2:
                    P.dma('sp', xbcT[:, t0:t0 + T].rearrange("(c p) t -> p c t", p=128), xb3[:, :, :T],
                          [('xbcst', None)], [('xbcT', ti)], 'st_xbc')
            P.dma('sp', qT_d[:, t0:t0 + T].rearrange("(c p) t -> p c t", p=128), q3[:, 0:8, :T],
                  [('qst', None)], [('qT', ti)], 'st_q')
            P.dma('sp', kT_d[:, t0:t0 + T].rearrange("(c p) t -> p c t", p=128), q3[:, 8:10, :T],
                  [('qst', None)], [('kT', ti)], 'st_q')
            nsub = T // 128
            zst = sqz.rearrange("p (s f) -> p s f", f=1024)
            vs3 = vst.rearrange("p (s f) -> p s f", f=256)
            ds3 = dst.rearrange("p (s f) -> p s f", f=32)
            for b in range(3):
                w, kw = pf.get()
                w3 = w.rearrange("p (k o) -> p k o", o=512)
                ncol = 512 if b < 2 else 288
                for sub in range(nsub):
                    pso, kp = P.ps()
                    for kc in range(KC):
                        P.mm(pso[:, :ncol], h3[:, kc, sub * 128:(sub + 1) * 128], w3[:, kc, 0:ncol], kc == 0,
                             kc == KC - 1, [kw, ('h', kc)], [kp])
                    if b < 2:
                        P.act(zst[:, sub, b * 512:(b + 1) * 512], pso[:, :], AF.Silu, [kp], [('sqz', ('z', sub, b))])
                    else:
                        P.cp('dve', vs3[:, sub, :], pso[:, 0:256], [kp], ['vst'])
                        P.tt('dve', dtt, pso[:, 256:288], vecB[:, 32:64], ALU.add, [kp, 'vecB'], ['dtt'])
                        P.act(dtt, dtt, AF.Exp, ['dtt'], ['dtt'])
                        P.act(ds3[:, sub, :], dtt, AF.Ln, ['dtt'], ['dst'], bias=1.0, scale=1.0)
            tsl = slice(t0, t0 + T)
            P.dma('sp', zs_d[tsl, :].rearrange("(s p) f -> p s f", p=128), zst[:, :nsub, :], [('sqz', None)],
                  [('zs', ti)], 'st_z')
            P.dma('sp', vtok_d[tsl, :].rearrange("(s p) f -> p s f", p=128), vs3[:, :nsub, :], ['vst'],
                  [('vtok', ti)], 'st_z')
            P.dma('sp', dts_d[tsl, :].rearrange("(s p) f -> p s f", p=128), ds3[:, :nsub, :], ['dst'],
                  [('dts', ti)], 'st_z')

    def phase_conv(l):
        P.bar()
        P.aoff = persist_end
        ub = [P.alloc(12 * 516), P.alloc(12 * 516)]
        acc = [P.alloc(512) for _ in range(4)]
        xcs = P.alloc_bf(12 * 512)
        xtk = P.alloc_bf(4 * 1280)
        cw = vecF[:, 64:124].rearrange("p (c j) -> p c j", j=5)
        cb = vecF[:, 124:136]
        nt = len(tiles)

        def load_u(ti):
            t0, T, isctx = tiles[ti]
            u3 = ub[ti % 2].rearrange("p (c t) -> p c t", t=516)
            first = isctx or ti == 1
            last = isctx or ti == nt - 1
            lo = 0 if first else 2
            hi = 0 if last else 2
            ku = f'u{ti % 2}'
            if first:
                P.memset('pool', u3[:, :, 0:2], 0.0, [(ku, 'l')])
            if last:
                P.memset('pool', u3[:, :, T + 2:T + 4], 0.0, [(ku, 'r')])
            P.dma('sp', u3[:, :, 2 - lo:T + 2 + hi],
                  xbcT[:, t0 - lo:t0 + T + hi].rearrange("(c p) t -> p c t", p=128),
                  [('xbcT', None)], [(ku, 'm')], f'ld_u{ti % 2}')

        load_u(0)
        ai = 0
        for ti, (t0, T, isctx) in enumerate(tiles):
            if ti + 1 < nt:
                load_u(ti + 1)
            ku = f'u{ti % 2}'
            u3 = ub[ti % 2].rearrange("p (c t) -> p c t", t=516)
            x3 = xcs.rearrange("p (c t) -> p c t", t=512)
            for c in range(12):
                eng = 'dve'
                a = acc[ai % 4]
                ka = f'acc{ai % 4}'
                ai += 1
                P.ts(eng, a[:, :T], u3[:, c, 0:T], cw[:, c, 0:1], None, ALU.mult, None, [(ku, None), 'vecF'], [ka])
                for jj in range(1, 5):
                    P.stt(eng, a[:, :T], u3[:, c, jj:jj + T], cw[:, c, jj:jj + 1], a[:, :T], ALU.mult, ALU.add,
                          [(ku, None), 'vecF', ka], [ka])
                P.act(x3[:, c, :T], a[:, :T], AF.Silu, [ka, 'vecF'], [('xcs', c)], bias=cb[:, c:c + 1], scale=1.0)
            P.dma('sp', xcT_d[:, t0:t0 + T].rearrange("(c p) t -> p c t", p=128), x3[:, :, :T], [('xcs', None)],
                  [('xcT', ti)], 'st_xc')
            nsub = T // 128
            xt3 = xtk.rearrange("p (s f) -> p s f", f=1280)
            for sub in range(nsub):
                pA, kA = P.ps()
                pB, kB = P.ps()
                pAb = pA.bitcast(BF16)
                pBb = pB.bitcast(BF16)
                for c in range(8):
                    P.tr(pAb[:, c * 128:(c + 1) * 128], x3[:, c, sub * 128:(sub + 1) * 128], ident_b,
                         [('xcs', c), 'cbf'], [kA])
                for c in range(2):
                    P.tr(pBb[:, c * 128:(c + 1) * 128], x3[:, 8 + c, sub * 128:(sub + 1) * 128], ident_b,
                         [('xcs', 8 + c), 'cbf'], [kB])
                P.cp('dve', xt3[:, sub, 0:1024], pAb[:, 0:1024], [kA], [('xtk', sub)])
                P.cp('act', xt3[:, sub, 1024:1280], pBb[:, 0:256], [kB], [('xtk', sub)])
            P.dma('sp', xtok_d[t0:t0 + T, :].rearrange("(s p) f -> p s f", p=128), xt3[:, :nsub, :],
                  [('xtk', None)], [('xtok', ti)], 'st_xc')

    def phase_ssd(l, d):
        P.bar()
        P.aoff = persist_end
        NBUF = 2
        xtkb = [P.alloc_bf(1280) for _ in range(NBUF)]
        bctb = [P.alloc_bf(512) for _ in range(NBUF)]
        dtb = [P.alloc(32) for _ in range(NBUF)]
        zsb = [P.alloc(1024) for _ in range(NBUF)]
        yfb = [P.alloc(1024) for _ in range(NBUF)]
        hs = P.alloc(1024)
        hsb = P.alloc_bf(1024)
        ld = P.alloc(16)
        negacs = P.alloc(16)
        eacs = P.alloc(16)
        dtea = P.alloc(16)
        dte = P.alloc(16)
        cd = P.alloc(16)
        dec = [P.alloc(128) for _ in range(4)]
        MT = [P.alloc_bf(128) for _ in range(4)]
        xd = P.alloc_bf(1024)
        xdd = P.alloc_bf(1024)
        yacc = P.alloc(1024)
        ytmp = P.alloc(1024)
        ybf = P.alloc_bf(1024)
        yTs = P.alloc_bf(1024)
        ssq = P.alloc(2)
        junk = P.alloc(1024)
        P.memset('dve', hs, 0.0, ['hs'])
        P.memset('dve', hsb, 0.0, ['hsb'])
        if d == 0:
            order = [0, 1] + [2 + i for i in range(nlc)]
        else:
            order = [1, 0] + [2 + i for i in reversed(range(nlc))]
        llast = 127 if d == 0 else 0

        def load(i):
            cb_ = order[i]
            tc = cb_ * 128
            s = i % NBUF
            P.dma('sp', xtkb[s], xtok_d[tc:tc + 128, :], [('xtok', None)], [f'xtk{s}'], f'ld_a{s}')
            P.dma('sp', bctb[s].rearrange("p (c t) -> p c t", t=128),
                  xcT_d[1024:1536, tc:tc + 128].rearrange("(c p) t -> p c t", p=128), [('xcT', None)],
                  [f'bct{s}'], f'ld_a{s}')
            P.dma('sp', dtb[s], dts_d[tc:tc + 128, :], [('dts', None)], [f'dt{s}'], f'ld_a{s}')
            if d == 1:
                P.dma('sp', zsb[s], zs_d[tc:tc + 128, :], [('zs', None)], [f'zs{s}'], f'ld_b{s}')
                P.dma('sp', yfb[s], yf_d[tc:tc + 128, :], [('yf', None)], [f'yf{s}'], f'ld_b{s}')

        load(0)
        di = 0
        for i, cb_ in enumerate(order):
            if i + 1 < len(order):
                load(i + 1)
            s = i % NBUF
            tc = cb_ * 128
            xt = xtkb[s]
            kxt = f'xtk{s}'
            bct = bctb[s].rearrange("p (c t) -> p c t", t=128)
            kbct = f'bct{s}'
            dt_d = dtb[s][:, 16 * d:16 * d + 16]
            kdt = f'dt{s}'
            P.tt('dve', ld, dt_d, a_bc[:, 16 * d:16 * d + 16], ALU.mult, [kdt, 'a_bc'], ['ld'])
            ps0, k0 = P.bank(0)
            psm = ps0[:, 256:272]
            P.mm(psm, tri[d], ld, True, True, ['cmat', 'ld'], [k0])
            for g in range(2):
                P.mm(ps0[:, g * 128:(g + 1) * 128], bct[:, g, :], bct[:, 2 + g, :], True, True, [kbct], [k0])
            P.ts('dve', negacs, psm, -1.0, None, ALU.mult, None, [k0], ['negacs'])
            P.act(eacs, psm, AF.Exp, [k0], ['eacs'])
            P.tt('dve', xd.rearrange("p (h q) -> p h q", q=64), xt[:, 0:1024].rearrange("p (h q) -> p h q", q=64),
                 dt_d.unsqueeze(2).to_broadcast([128, 16, 64]), ALU.mult, [kxt, kdt], ['xd'])
            for g in range(2):
                po, ko = P.bank(5 + g)
                P.mm(po, bct[:, 2 + g, :], hsb[:, g * 512:(g + 1) * 512], True, True, [kbct, 'hsb'], [ko])
                sl = slice(g * 512, (g + 1) * 512)
                P.tt('dve', ytmp[:, sl].rearrange("p (h q) -> p h q", q=64), po.rearrange("p (h q) -> p h q", q=64),
                     eacs[:, g * 8:(g + 1) * 8].unsqueeze(2).to_broadcast([128, 8, 64]), ALU.mult, [ko, 'eacs'],
                     [('ytmp', g)])
            for r in range(2):
                pas = []
                for q in range(2):
                    pa, ka = P.bank(1 + q)
                    pas.append((pa, ka))
                    for hh in range(4):
                        h = r * 8 + q * 4 + hh
                        P.mm(pa[:, hh * 128:(hh + 1) * 128], ld[:, h:h + 1].to_broadcast([128, 128]), tri[d], True,
                             False, ['ld', 'cmat'], [ka])
                        P.mm(pa[:, hh * 128:(hh + 1) * 128], ident_f, mbias[d], False, True, ['cmat'], [ka])
                for q in range(2):
                    pa, ka = pas[q]
                    h0 = r * 8 + q * 4
                    lastcol = pa.rearrange("p (h t) -> p h t", t=128)[:, :, llast]
                    P.tt('dve', dtea[:, h0:h0 + 4], lastcol, negacs[:, h0:h0 + 4], ALU.add, [ka, 'negacs'],
                         [('dtea', h0)])
                    P.act(cd[:, h0:h0 + 4], lastcol, AF.Exp, [ka], [('cd', h0)])
                py, ky = P.bank(3 + r)
                for q in range(2):
                    pa, ka = pas[q]
                    for hh in range(4):
                        h = r * 8 + q * 4 + hh
                        dc = dec[di % 4]
                        kdc = f'dec{di % 4}'
                        m_ = MT[di % 4]
                        kmt = f'MT{di % 4}'
                        di += 1
                        P.act(dc, pa[:, hh * 128:(hh + 1) * 128], AF.Exp, [ka, 'negacs'], [kdc],
                              bias=negacs[:, h:h + 1], scale=1.0)
                        P.tt('dve', m_, ps0[:, r * 128:(r + 1) * 128], dc, ALU.mult, [k0, kdc], [kmt])
                        h8 = h % 8
                        P.mm(py[:, h8 * 64:(h8 + 1) * 64], m_, xd[:, h * 64:(h + 1) * 64], True, True, [kmt, 'xd'],
                             [ky])
                sl = slice(r * 512, (r + 1) * 512)
                P.tt('dve', yacc[:, sl], ytmp[:, sl], py, ALU.add, [('ytmp', r), ky], [('yacc', r)])
            P.act(dte, dtea, AF.Exp, [('dtea', None)], ['dte'])
            P.tt('dve', xdd.rearrange("p (h q) -> p h q", q=64), xd.rearrange("p (h q) -> p h q", q=64),
                 dte.unsqueeze(2).to_broadcast([128, 16, 64]), ALU.mult, ['xd', 'dte'], ['xdd'])
            P.tt('dve', hs.rearrange("p (h q) -> p h q", q=64), hs.rearrange("p (h q) -> p h q", q=64),
                 cd.unsqueeze(2).to_broadcast([128, 16, 64]), ALU.mult, ['hs', ('cd', None)], ['hs'])
            for g in range(2):
                pS, kS = P.bank(5 + g)
                P.mm(pS, xt[:, 1024 + g * 128:1024 + (g + 1) * 128], xdd[:, g * 512:(g + 1) * 512], True, True,
                     [kxt, 'xdd'], [kS])
                P.tt('dve', hs[:, g * 512:(g + 1) * 512], hs[:, g * 512:(g + 1) * 512], pS, ALU.add, ['hs', kS],
                     ['hs'])
            P.cp('act', hsb, hs, ['hs'], ['hsb'])
            if d == 0:
                P.dma('sp', yf_d[tc:tc + 128, :], yacc, [('yacc', None)], [('yf', cb_)], 'st_yf')
            else:
                P.tt('dve', yacc, yacc, yfb[s], ALU.add, [('yacc', None), f'yf{s}'], [('yacc', None)])
                P.tt('pool', ytmp.rearrange("p (h q) -> p h q", q=64), xt[:, 0:1024].rearrange("p (h q) -> p h q", q=64),
                     vecB[:, 64:80].unsqueeze(2).to_broadcast([128, 16, 64]), ALU.mult, [kxt, 'vecB'],
                     [('ytmp', None)])
                P.tt('dve', yacc, yacc, ytmp, ALU.add, [('yacc', None), ('ytmp', None)], [('yacc', None)])
                P.tt('dve', yacc, yacc, zsb[s], ALU.mult, [('yacc', None), f'zs{s}'], [('yacc', None)])
                P.act(junk, yacc, AF.Square, [('yacc', None)], ['junk', 'ssq'], accum_out=ssq[:, 0:1])
                P.act(ssq[:, 1:2], ssq[:, 0:1], AF.Sqrt, ['ssq'], ['ssq2'], bias=EPS, scale=1.0 / D_SSM)
                P.recip(ssq[:, 1:2], ssq[:, 1:2], ['ssq2'], ['ssq2'])
                P.stt('dve', ybf, yacc, ssq[:, 1:2], vecB[:, 80:1104], ALU.mult, ALU.mult,
                      [('yacc', None), 'ssq2', 'vecB'], ['ybf'])
                pT, kT_ = P.bank(7)
                pTb = pT.bitcast(BF16)
                for c in range(8):
                    P.tr(pTb[:, c * 128:(c + 1) * 128], ybf[:, c * 128:(c + 1) * 128], ident_b, ['ybf', 'cbf'], [kT_])
                P.cp('act', yTs, pTb[:, 0:1024], [kT_], ['yTs'])
                P.dma('sp', yT_d[0:1024, tc:tc + 128].rearrange("(c p) t -> p c t", p=128),
                      yTs.rearrange("p (c t) -> p c t", t=128), ['yTs'], [('yT', ('s', cb_))], 'st_yT')

    def phase_attn(l, last):
        P.bar()
        P.aoff = persist_end
        kTg = P.alloc_bf(NT)
        Vg = P.alloc_bf(NT)
        q4b = [P.alloc_bf(2048), P.alloc_bf(2048)]
        pTb_ = [P.alloc_bf(512) for _ in range(6)]
        ost = [P.alloc_bf(2048), P.alloc_bf(2048)]
        den = [P.alloc(512), P.alloc(512)]
        scale = 1.0 / np.sqrt(128.0)
        qblocks = ([] if last else [0, 1]) + [2 + i for i in range(nlc)]
        groups = []
        if not last:
            groups.append([0, 1])
        for i in range(0, nlc, 4):
            groups.append([2 + i + k for k in range(4)])
        pi = 0
        for g in range(2):
            P.dma('sp', kTg, kT_d[g * 128:(g + 1) * 128, :], [('kT', None)], ['kTg'], 'ld_kv')
            V3 = Vg.rearrange("p (b d) -> p b d", d=128)
            P.dma('sp', V3, vtok_d[:, g * 128:(g + 1) * 128].rearrange("(b p) d -> p b d", p=128),
                  [('vtok', None)], ['Vg'], 'ld_kv')
            for gi, grp in enumerate(groups):
                s = gi % 2
                nq = len(grp)
                tq0 = grp[0] * 128
                q4 = q4b[s].rearrange("p (h t) -> p h t", t=512)
                P.dma('sp', q4[:, :, :nq * 128],
                      qT_d[g * 512:(g + 1) * 512, tq0:tq0 + nq * 128].rearrange("(h p) t -> p h t", p=128),
                      [('qT', None)], [f'q4{s}'], f'ld_q{s}')
                o4 = ost[s].rearrange("p (h t) -> p h t", t=512)
                for qi, qb in enumerate(grp):
                    if qb < 2:
                        keys = [(0, None), (1, None)]
                    else:
                        n = qb - 2
                        keys = []
                        if n > 0:
                            keys.append((qb - 1, 0))
                        keys.append((qb, None))
                        if n < nlc - 1:
                            keys.append((qb + 1, 1))
                        keys += [(0, None), (1, None)]
                    rq = q4[:, :, qi * 128:(qi + 1) * 128]
                    psO_, kO = P.ps()
                    psD, kD = P.ps()
                    for ki, (kb, mk) in enumerate(keys):
                        psS_, kS = P.ps()
                        P.mm(psS_.rearrange("p (h t) -> p h t", t=128), kTg[:, kb * 128:(kb + 1) * 128], rq, True,
                             mk is None, ['kTg', f'q4{s}'], [kS])
                        if mk is not None:
                            P.mm(psS_, ident_b, amask[:, mk * 512:(mk + 1) * 512], False, True, ['cbf', 'amask'], [kS])
                        pt = pTb_[pi % 6]
                        kpt = f'pT{pi % 6}'
                        pi += 1
                        P.act(pt, psS_, AF.Exp, [kS], [kpt], scale=float(scale))
                        P.mm(psO_, V3[:, kb, :], pt, ki == 0, ki == len(keys) - 1, ['Vg', kpt], [kO])
                        P.mm(psD, ones_b, pt, ki == 0, ki == len(keys) - 1, ['cbf', kpt], [kD])
                    dn = den[qi % 2]
                    kdn = f'den{qi % 2}'
                    P.tt('dve', dn, psD, expsink[:, g * 512:(g + 1) * 512], ALU.add, [kD, 'expsink'], [kdn])
                    P.recip(dn, dn, [kdn], [kdn])
                    P.tt('dve', o4[:, :, qi * 128:(qi + 1) * 128], psO_.rearrange("p (h t) -> p h t", t=128),
                         dn.rearrange("p (h t) -> p h t", t=128), ALU.mult, [kO, kdn], [(f'ost{s}', qi)])
                P.dma('sp', yT_d[1024 + g * 512:1024 + (g + 1) * 512, tq0:tq0 + nq * 128].rearrange(
                    "(h p) t -> p h t", p=128), o4[:, :, :nq * 128], [(f'ost{s}', None)], [('yT', ('a', g, gi))],
                    'st_yT')

    def phase_mlp(l, last):
        P.bar()
        P.aoff = persist_end
        xb = P.alloc(8192)
        mixs = P.alloc(8192)
        bufA = P.alloc_bf(8192)
        ub = P.alloc_bf(32 * 512)
        ycat = P.alloc_bf(8192)
        rstd = P.alloc(512)
        tmp = [P.alloc(512) for _ in range(3)]
        rl = [P.alloc(512) for _ in range(2)]
        rl.append(rl[0])
        xsrc = xT0 if l == 0 else xTs
        my_tiles = [t for t in tiles if not (last and t[2])]
        seq = []
        for _ in my_tiles:
            seq += [blk(l, OFF_WOUT, i) for i in range(4)]
            for hf in range(2):
                seq += [blk(l, OFF_FF1, hf * 8 + i) for i in range(8)]
                seq += [blk(l, OFF_FF2, hf * 8 + i) for i in range(8)]
        pf = Prefetch(seq)
        ti_ = [0]

        def nxt(lst, nm):
            i = ti_[0] % (3 if nm == 'tmp' else 2)
            ti_[0] += 1
            return lst[i], f'{nm}{i}'

        for (t0, T, isctx) in my_tiles:
            j = 1 if isctx else 0
            x3 = xb.rearrange("p (c t) -> p c t", t=512)[:, :, :T]
            m3 = mixs.rearrange("p (c t) -> p c t", t=512)
            a3 = bufA.rearrange("p (c t) -> p c t", t=512)
            y3 = ycat.rearrange("p (c t) -> p c t", t=512)
            u3 = ub.rearrange("p (c t) -> p c t", t=512)
            P.dma('sp', y3[:, :, :T], yT_d[:, t0:t0 + T].rearrange("(c p) t -> p c t", p=128), [('yT', None)],
                  ['ycat'], 'ld_y')
            P.dma('sp', x3, xsrc[:, t0:t0 + T].rearrange("(c p) t -> p c t", p=128),
                  [('xT', None)] if l > 0 else [], ['xb'], 'ld_xb')
            for b in range(4):
                w, kw = pf.get()
                w3 = w.rearrange("p (k o) -> p k o", o=512)
                for oc in range(4):
                    o = b * 4 + oc
                    pso, kp = P.ps()
                    for kc in range(KC):
                        P.mm(pso[:, :T], w3[:, kc, oc * 128:(oc + 1) * 128], y3[:, kc, :T], kc == 0, kc == KC - 1,
                             [kw, 'ycat'], [kp])
                    if o % 2 == 0:
                        P.cp('act', m3[:, o, :T], pso[:, :T], [kp], [('mixs', o)])
                    else:
                        P.cp('dve', m3[:, o, :T], pso[:, :T], [kp], [('mixs', o)])
            rms_rstd(m3[:, :, :T], ('mixs', None), a3, 'bufA', rstd, 'rstd', T)
            for c in range(KC):
                tm, ktm = nxt(tmp, 'tmp')
                P.tt('dve', tm[:, :T], m3[:, c, :T], rstd[:, :T], ALU.mult, [('mixs', c), 'rstd'], [ktm])
                P.stt('dve', x3[:, c, :], tm[:, :T], G_m[:, c, j:j + 1], x3[:, c, :], ALU.mult, ALU.add,
                      [ktm, 'der', ('xb', c)], [('xb', c)])
            rms_rstd(x3, ('xb', None), a3, 'bufA', rstd, 'rstd', T)
            for c in range(KC):
                tm, ktm = nxt(tmp, 'tmp')
                P.stt('dve', tm[:, :T], x3[:, c, :], A_f[:, c, j:j + 1], rstd[:, :T], ALU.mult, ALU.mult,
                      [('xb', c), 'der', 'rstd'], [ktm])
                P.act(a3[:, c, :T], tm[:, :T], AF.Identity, [ktm, 'mod'], [('bufA', c)], bias=B_f[:, c, j:j + 1],
                      scale=1.0)
            for hf in range(2):
                for b in range(8):
                    w, kw = pf.get()
                    w3 = w.rearrange("p (k o) -> p k o", o=512)
                    for oc in range(4):
                        fcl = b * 4 + oc
                        pso, kp = P.ps()
                        for kc in range(KC):
                            P.mm(pso[:, :T], w3[:, kc, oc * 128:(oc + 1) * 128], a3[:, kc, :T], kc == 0, kc == KC - 1,
                                 [kw, ('bufA', kc)], [kp])
                        r, kr = nxt(rl, 'rl')
                        P.act(r[:, :T], pso[:, :T], AF.Relu, [kp], [kr])
                        P.tt('pool', u3[:, fcl, :T], r[:, :T], r[:, :T], ALU.mult, [kr], [('ub', fcl)])
                for b in range(8):
                    w, kw = pf.get()
                    w3 = w.rearrange("p (k o) -> p k o", o=256)
                    for oc in range(2):
                        o = b * 2 + oc
                        pso, kp = P.ps()
                        for fc in range(32):
                            P.mm(pso[:, :T], w3[:, fc, oc * 128:(oc + 1) * 128], u3[:, fc, :T], fc == 0, fc == 31,
                                 [kw, ('ub', fc)], [kp])
                        if hf == 0:
                            P.cp('act', m3[:, o, :T], pso[:, :T], [kp], [('mixs', o)])
                        else:
                            P.tt('dve', m3[:, o, :T], m3[:, o, :T], pso[:, :T], ALU.add, [('mixs', o), kp],
                                 [('mixs', o)])
            rms_rstd(m3[:, :, :T], ('mixs', None), a3, 'bufA', rstd, 'rstd', T)
            for c in range(KC):
                tm, ktm = nxt(tmp, 'tmp')
                P.tt('dve', tm[:, :T], m3[:, c, :T], rstd[:, :T], ALU.mult, [('mixs', c), 'rstd'], [ktm])
                P.stt('dve', x3[:, c, :], tm[:, :T], G_f[:, c, j:j + 1], x3[:, c, :], ALU.mult, ALU.add,
                      [ktm, 'der', ('xb', c)], [('xb', c)])
            if last:
                P.dma('sp', outT[:, t0 - LC:t0 - LC + T].rearrange("(c p) t -> p c t", p=128), x3, [('xb', None)],
                      [('outT', t0)], 'st_x')
            else:
                P.dma('sp', xTs[:, t0:t0 + T].rearrange("(c p) t -> p c t", p=128), x3, [('xb', None)],
                      [('xT', t0)], 'st_x')

    phases = []
    for l in range(NL):
        last = (l == NL - 1)
        phases += [lambda l=l: layer_setup(l), lambda l=l: phase_mod(l), lambda l=l: phase_inproj(l),
                   lambda l=l: phase_conv(l), lambda l=l: phase_ssd(l, 0), lambda l=l: phase_ssd(l, 1),
                   lambda l=l, last=last: phase_attn(l, last), lambda l=l, last=last: phase_mlp(l, last)]
    if stop_after is not None:
        phases = phases[:stop_after]
    for ph in phases:
        ph()
    P.bar()
    P.op('sp', lambda e: e.nop(), (), ())
    P.S.emit()
    return nc


def _pack_k2048(w):
    K, C = w.shape
    nb = C // 512
    return np.ascontiguousarray(w.reshape(16, 128, nb, 512).transpose(2, 1, 0, 3)).reshape(nb, 128, 8192)


def _fm(v, nch):
    return np.ascontiguousarray(v.reshape(nch, 128).T)


def pack_weights(inp, NL):
    wall = np.zeros((NL * NBLK_L, 128, 8192), np.float32)
    vecF = np.zeros((NL, 128, NVF), np.float32)
    vecB = np.zeros((NL, 1, NVB), np.float32)
    for l in range(NL):
        base = l * NBLK_L
        wall[base + OFF_WMOD:base + OFF_WMOD + 24] = _pack_k2048(inp['w_mod'][l])
        wi = inp['w_in'][l]
        wF = np.zeros((2048, 3072), np.float32)
        wF[:, 0:1536] = wi[:, 0:1536]
        wF[:, 1536:2560] = wi[:, 2592:3616]
        wF[:, 2560:2816] = wi[:, 3616:3872]
        wall[base + OFF_WINF:base + OFF_WINF + 6] = _pack_k2048(wF)
        wT = np.zeros((2048, 1536), np.float32)
        wT[:, 0:1024] = wi[:, 1536:2560]
        wT[:, 1024:1280] = wi[:, 3872:4128]
        wT[:, 1280:1312] = wi[:, 2560:2592]
        wall[base + OFF_WINT:base + OFF_WINT + 3] = _pack_k2048(wT)
        wall[base + OFF_WOUT:base + OFF_WOUT + 4] = _pack_k2048(inp['w_out'][l])
        wall[base + OFF_FF1:base + OFF_FF1 + 16] = _pack_k2048(inp['w_ff1'][l])
        w2 = inp['w_ff2'][l]
        wall[base + OFF_FF2:base + OFF_FF2 + 16] = np.ascontiguousarray(
            w2.reshape(2, 32, 128, 8, 256).transpose(0, 3, 2, 1, 4)).reshape(16, 128, 8192)
        vecF[l, :, 0:16] = _fm(inp['g_pre_mix'][l], 16)
        vecF[l, :, 16:32] = _fm(inp['g_post_mix'][l], 16)
        vecF[l, :, 32:48] = _fm(inp['g_pre_mlp'][l], 16)
        vecF[l, :, 48:64] = _fm(inp['g_post_mlp'][l], 16)
        cw = inp['conv_w'][l]
        vecF[l, :, 64:124] = np.ascontiguousarray(cw.T.reshape(12, 128, 5).transpose(1, 0, 2)).reshape(128, 60)
        vecF[l, :, 124:136] = _fm(inp['conv_b'][l], 12)
        vecF[l, :, 136:232] = _fm(inp['b_mod'][l], 96)
        vecB[l, 0, 0:32] = inp['a_log'][l].reshape(32)
        vecB[l, 0, 32:64] = inp['dt_bias'][l].reshape(32)
        vecB[l, 0, 64:80] = inp['d_skip'][l]
        vecB[l, 0, 80:1104] = inp['ssm_norm'][l]
        vecB[l, 0, 1104:1112] = inp['attn_sink'][l]
    return wall, vecF, vecB


def make_consts(L):
    cm = np.zeros((128, 7 * 128), np.float32)
    i = np.arange(128)
    cm[:, 0:128] = np.eye(128)
    cm[:, 128:256] = (i[:, None] <= i[None, :])
    cm[:, 256:384] = (i[:, None] >= i[None, :])
    cm[:, 384:512] = np.where(i[:, None] <= i[None, :], 0.0, -BIG)
    cm[:, 512:640] = np.where(i[:, None] >= i[None, :], 0.0, -BIG)
    partner = np.where((i // 32) % 2 == 0, i + 32, i - 32)
    rot = np.zeros((128, 128), np.float32)
    rot[partner, i] = 1.0
    cm[:, 640:768] = rot
    cm[:, 768:896] = 1.0
    am = np.zeros((128, 1024), np.float32)
    prev = np.where(i[:, None] >= i[None, :], 0.0, -BIG)
    nxt = np.where(i[:, None] <= i[None, :], 0.0, -BIG)
    am[:, 0:512] = np.tile(prev, (1, 4))
    am[:, 512:1024] = np.tile(nxt, (1, 4))
    nf = 32
    inv = (10000.0 ** (-np.arange(nf, dtype=np.float32) / nf)).astype(np.float32)
    t = np.arange(L)
    pos_row = (t // 64).astype(np.float32)
    pos_col = (t % 64).astype(np.float32)
    ang_r = pos_row[None, :] * inv[:, None]
    ang_c = pos_col[None, :] * inv[:, None]
    cos = np.concatenate([np.cos(ang_r), np.cos(ang_r), np.cos(ang_c), np.cos(ang_c)], 0)
    sin = np.concatenate([-np.sin(ang_r), np.sin(ang_r), -np.sin(ang_c), np.sin(ang_c)], 0)
    rope = np.stack([cos, sin]).astype(np.float32)
    return cm, am, rope


def make_in_map(inp, b, L, shared):
    wall, vecF, vecB, cm, am, rope = shared
    xT = np.ascontiguousarray(np.concatenate([inp['ctx'][b], inp['x'][b][:L]], 0).T)
    ccv = np.stack([_fm(inp['c'][b], 16), _fm(inp['c_ctx'], 16)], -1).reshape(128, 32)
    return {"xT0": xT, "cc": np.ascontiguousarray(ccv), "wall": wall, "vecF": vecF, "vecB": vecB, "cmat": cm,
            "amask": am, "rope": rope}


def kernel(**inputs):
    inp = {k: np.asarray(v, dtype=np.float32) for k, v in inputs.items()}
    L = inp['x'].shape[1]
    NL = inp['w_in'].shape[0]
    B = inp['x'].shape[0]
    nc = build_program(L, NL)
    shared = pack_weights(inp, NL) + make_consts(L)
    in_maps = [make_in_map(inp, c % B, L, shared) for c in range(8)]
    res = run_bass_kernel_spmd(nc, in_maps, core_ids=list(range(8)))
    out = np.stack([np.ascontiguousarray(res.results[b]["outT"].T) for b in range(B)], 0)
    return out.astype(np.float32)
```

```python
import numpy as np
import concourse.bass as bass
import concourse.mybir as mybir
from concourse.bass_utils import run_bass_kernel_spmd

F32 = mybir.dt.float32
BF16 = mybir.dt.bfloat16
AF = mybir.ActivationFunctionType
ALU = mybir.AluOpType

D = 2048
KC = 16
LC = 256
D_SSM = 1024
D_XBC = 1536
NH = 16
HP = 64
NST = 128
D_FF = 8192
EPS = 1e-6
BIG = 30000.0
NBLK_L = 69
OFF_WMOD, OFF_WINF, OFF_WINT, OFF_WOUT, OFF_FF1, OFF_FF2 = 0, 24, 30, 33, 37, 53
NVF = 232
NVB = 1112
ENG = ['pe', 'act', 'dve', 'pool', 'sp']


class Sched:
    def __init__(self, nc):
        self.nc = nc
        self.ops = {e: [] for e in ENG}
        self.lastw = {}
        self.readers = {}
        self.seen = {e: {} for e in ENG}
        self.seend = {e: {} for e in ENG}
        self.chan_cnt = {}

    @staticmethod
    def _norm(b):
        return b if isinstance(b, tuple) else (b, None)

    def _deps(self, reads, writes):
        deps = []
        for (b, k) in reads:
            lw = self.lastw.get(b, {})
            toks = list(lw.values()) if k is None else [lw[x] for x in (k, None) if x in lw]
            deps += [(t, 'raw') for t in toks]
        for (b, k) in writes:
            lw = self.lastw.get(b, {})
            rd = self.readers.get(b, {})
            if k is None:
                deps += [(t, 'waw') for t in lw.values()]
                for d_ in rd.values():
                    deps += [(t, 'war') for t in d_.values()]
            else:
                for x in (k, None):
                    if x in lw:
                        deps.append((lw[x], 'waw'))
                    if x in rd:
                        deps += [(t, 'war') for t in rd[x].values()]
        return deps

    def _commit(self, tok, reads, writes):
        for (b, k) in reads:
            d_ = self.readers.setdefault(b, {}).setdefault(k, {})
            d_[(tok[0], tok[1])] = tok
        for (b, k) in writes:
            if k is None:
                self.lastw[b] = {None: tok}
                self.readers[b] = {}
            else:
                self.lastw.setdefault(b, {})[k] = tok
                self.readers.setdefault(b, {})[k] = {}

    def op(self, eng, fn, reads=(), writes=(), chan=None):
        reads = [self._norm(r) for r in reads]
        writes = [self._norm(w) for w in writes]
        deps = self._deps(reads, writes)
        idx = len(self.ops[eng])
        waits = []
        for (tok, kind) in deps:
            if tok[0] == 'e':
                _, A, i = tok
                if A == eng and (eng == 'pe' or kind != 'raw'):
                    continue
                if self.seen[eng].get(A, -1) >= i:
                    continue
                self.seen[eng][A] = i
                waits.append(tok)
                self.ops[A][i]['sig'] = True
            else:
                _, ch, cnt = tok
                cnt = self.chan_cnt[ch]
                if self.seend[eng].get(ch, 0) >= cnt:
                    continue
                self.seend[eng][ch] = cnt
                waits.append(('d', ch, cnt))
        self.ops[eng].append(dict(fn=fn, waits=waits, sig=False, chan=chan))
        if chan is None:
            tok = ('e', eng, idx)
        else:
            self.chan_cnt[chan] = self.chan_cnt.get(chan, 0) + 16
            tok = ('d', chan, self.chan_cnt[chan])
        self._commit(tok, reads, writes)

    def barrier(self):
        for e in ENG:
            for A in ENG:
                if A == e or not self.ops[A]:
                    continue
                i = len(self.ops[A]) - 1
                while i >= 0 and self.ops[A][i]['chan'] is not None:
                    i -= 1
                if i < 0 or self.seen[e].get(A, -1) >= i:
                    continue
                self.seen[e][A] = i
                self.ops[A][i]['sig'] = True
                self._pending_bar.setdefault(e, []).append(('e', A, i))
            for ch, cnt in self.chan_cnt.items():
                if self.seend[e].get(ch, 0) >= cnt:
                    continue
                self.seend[e][ch] = cnt
                self._pending_bar.setdefault(e, []).append(('d', ch, cnt))

    _pending_bar = None

    def emit(self):
        nc = self.nc
        sems = {e: nc.alloc_semaphore(f"s_{e}") for e in ENG}
        chsem = {ch: nc.alloc_semaphore(f"c_{ch}") for ch in self.chan_cnt}
        for e in ENG:
            c = 0
            for o in self.ops[e]:
                if o['chan'] is None and o['sig']:
                    c += 1
                    o['sv'] = c
        ops = self.ops
        chan_cnt = self.chan_cnt

        def run(e, eng):
            for o in ops[e]:
                best = {}
                for w in o['waits']:
                    if w[0] == 'e':
                        key = ('e', w[1])
                        val = ops[w[1]][w[2]]['sv']
                    else:
                        key = ('d', w[1])
                        val = w[2]
                    best[key] = max(best.get(key, 0), val)
                for key, val in best.items():
                    sem = sems[key[1]] if key[0] == 'e' else chsem[key[1]]
                    eng.wait_ge(sem, val)
                ins = o['fn'](eng)
                if o['chan'] is not None:
                    ins.then_inc(chsem[o['chan']], 16)
                elif o['sig']:
                    ins.then_inc(sems[e], 1)
            if e == 'sp':
                for ch, cnt in chan_cnt.items():
                    eng.wait_ge(chsem[ch], cnt)

        with nc.Block() as block:
            block.tensor(lambda t: run('pe', t))
            block.scalar(lambda t: run('act', t))
            block.vector(lambda t: run('dve', t))
            block.gpsimd(lambda t: run('pool', t))
            block.sync(lambda t: run('sp', t))


class Prog:
    def __init__(self, L, NL, dbg=()):
        self.L = L
        self.NL = NL
        self.NT = LC + L
        self.dbg = set(dbg)
        self.nc = bass.Bass("TRN2", target_bir_lowering=False)
        self.S = Sched(self.nc)
        self.S._pending_bar = {}
        nc = self.nc
        self.arena = nc.alloc_sbuf_tensor("arena", [128, 53000], F32)
        self.aoff = 0
        self.psum = nc.alloc_psum_tensor("psum", [128, 4096], F32)
        self.ps_i = 0
        self.ring_i = 0
        self.NS = 3

    def alloc(self, n_f32):
        o = self.aoff
        self.aoff += (n_f32 + 7) // 8 * 8
        assert self.aoff <= 53000, self.aoff
        return self.arena[:, o:o + n_f32]

    def alloc_bf(self, n_bf16):
        return self.alloc((n_bf16 + 1) // 2).bitcast(BF16)

    ps_n = 8

    def ps(self):
        b = self.ps_i % self.ps_n
        self.ps_i += 1
        return self.psum[:, b * 512:(b + 1) * 512], ('ps', b)

    def bank(self, b):
        return self.psum[:, b * 512:(b + 1) * 512], ('ps', b)

    def bar(self):
        S = self.S
        S._pending_bar = {}
        S.barrier()
        pend = S._pending_bar
        for e, lst in pend.items():
            self._prewaits.setdefault(e, []).extend(lst)

    _prewaits = None

    def op(self, eng, fn, reads=(), writes=(), chan=None):
        self.S.op(eng, fn, reads, writes, chan)
        if self._prewaits and eng in self._prewaits:
            self.S.ops[eng][-1]['waits'].extend(self._prewaits.pop(eng))

    def mm(self, out, lhsT, rhs, start, stop, reads, writes):
        self.op('pe', lambda e: e.matmul(out, lhsT, rhs, start=start, stop=stop), reads, writes)

    def tr(self, out, in_, ident, reads, writes):
        self.op('pe', lambda e: e.transpose(out, in_, ident), reads, writes)

    def act(self, out, in_, func, reads, writes, bias=None, scale=None, accum_out=None, eng='act'):
        kw = {}
        if bias is not None:
            kw['bias'] = bias
        if scale is not None:
            kw['scale'] = scale
        if accum_out is not None:
            kw['accum_out'] = accum_out
        self.op(eng, lambda e: e.activation(out=out, in_=in_, func=func, **kw), reads, writes)

    def tt(self, eng, out, in0, in1, op, reads, writes):
        self.op(eng, lambda e: e.tensor_tensor(out=out, in0=in0, in1=in1, op=op), reads, writes)

    def ts(self, eng, out, in0, s1, s2, op0, op1, reads, writes):
        if s2 is None:
            self.op(eng, lambda e: e.tensor_scalar(out=out, in0=in0, scalar1=s1, scalar2=None, op0=op0),
                    reads, writes)
        else:
            self.op(eng, lambda e: e.tensor_scalar(out=out, in0=in0, scalar1=s1, scalar2=s2, op0=op0, op1=op1),
                    reads, writes)

    def stt(self, eng, out, in0, scalar, in1, op0, op1, reads, writes):
        self.op(eng, lambda e: e.scalar_tensor_tensor(out=out, in0=in0, scalar=scalar, in1=in1, op0=op0, op1=op1),
                reads, writes)

    def cp(self, eng, out, in_, reads, writes):
        if eng == 'act':
            self.op(eng, lambda e: e.copy(out=out, in_=in_), reads, writes)
        else:
            self.op(eng, lambda e: e.tensor_copy(out=out, in_=in_), reads, writes)

    def recip(self, out, in_, reads, writes):
        self.op('dve', lambda e: e.reciprocal(out=out, in_=in_), reads, writes)

    def memset(self, eng, ap, val, writes):
        self.op(eng, lambda e: e.memset(ap, val), (), writes)

    def dma(self, eng, out, in_, reads, writes, chan):
        self.op(eng, lambda e: e.dma_start(out=out, in_=in_), reads, writes, chan)


def blk(l, off, i):
    return l * NBLK_L + off + i


WGRP = [(0, 8, 'm0'), (8, 16, 'm1'), (16, 24, 'm2'), (24, 33, 'in'), (33, 37, 'wo'),
        (37, 45, 'f1a'), (45, 53, 'f1b'), (53, 61, 'f2a'), (61, 69, 'f2b')]


def wgrp(b):
    l, r = divmod(b, NBLK_L)
    for lo, hi, nm in WGRP:
        if lo <= r < hi:
            return f"cv{l}{nm}"
    raise ValueError


def build_program(L=4096, NL=2, dbg=(), stop_after=None):
    P = Prog(L, NL, dbg)
    P._prewaits = {}
    nc = P.nc
    NT = P.NT
    NB = NT // 128
    nlat_tiles = L // 512
    tiles = [(0, 256, True)] + [(LC + i * 512, 512, False) for i in range(nlat_tiles)]
    nlc = L // 128

    def dram(name, shape, dt, kind="Internal"):
        if name in P.dbg:
            kind = "ExternalOutput"
        return nc.dram_tensor(name, shape, dt, kind=kind).ap()

    xT0 = dram("xT0", [D, NT], F32, "ExternalInput")
    cc_d = dram("cc", [128, KC * 2], F32, "ExternalInput")
    wall = dram("wall", [NL * NBLK_L, 128, 8192], F32, "ExternalInput")
    vecF_d = dram("vecF", [NL, 128, NVF], F32, "ExternalInput")
    vecB_d = dram("vecB", [NL, 1, NVB], F32, "ExternalInput")
    cmat_d = dram("cmat", [128, 7 * 128], F32, "ExternalInput")
    amask_d = dram("amask", [128, 2 * 512], F32, "ExternalInput")
    rope_d = dram("rope", [2, 128, L], F32, "ExternalInput")
    outT = dram("outT", [D, L], F32, "ExternalOutput")

    wbf_l = [dram(f"wbf{l_}", [NBLK_L, 128, 8192], BF16) for l_ in range(NL)]

    class _WB:
        def __getitem__(self, b):
            return wbf_l[b // NBLK_L][b % NBLK_L]
    wbf = _WB()
    xTs = dram("xTs", [D, NT], F32)
    xbcT = dram("xbcT", [D_XBC, NT], F32)
    zs_d = dram("zs", [NT, 1024], F32)
    dts_d = dram("dts", [NT, 32], F32)
    qT_d = dram("qT", [1024, NT], BF16)
    kT_d = dram("kT", [256, NT], BF16)
    vtok_d = dram("vtok", [NT, 256], BF16)
    xcT_d = dram("xcT", [D_XBC, NT], BF16)
    xtok_d = dram("xtok", [NT, 1280], BF16)
    yf_d = dram("yf", [NT, 1024], F32)
    yT_d = dram("yT", [D, NT], BF16)

    ring = [P.alloc_bf(8192) for _ in range(P.NS)]
    cmat = P.alloc(7 * 128)
    ident_f = cmat[:, 0:128]
    tri = [cmat[:, 128:256], cmat[:, 256:384]]
    mbias = [cmat[:, 384:512], cmat[:, 512:640]]
    rot_f = cmat[:, 640:768]
    cbf = P.alloc_bf(2 * 128)
    ident_b = cbf[:, 0:128]
    ones_b = cbf[:, 128:256]
    amask = P.alloc_bf(1024)
    cc = P.alloc(32)
    scb = P.alloc_bf(32)
    vecF = P.alloc(NVF)
    vecB = P.alloc(NVB)
    mod = P.alloc(192)
    der = P.alloc(4 * 32)
    a_bc = P.alloc(32)
    esink = P.alloc(8)
    expsink = P.alloc(1024)
    persist_end = P.aoff
    amask_f = P.alloc(1024)

    mod3 = mod.rearrange("p (c j) -> p c j", j=2)
    A_m = der[:, 0:32].rearrange("p (c j) -> p c j", j=2)
    G_m = der[:, 32:64].rearrange("p (c j) -> p c j", j=2)
    A_f = der[:, 64:96].rearrange("p (c j) -> p c j", j=2)
    G_f = der[:, 96:128].rearrange("p (c j) -> p c j", j=2)
    B_m = mod3[:, 0:16, :]
    B_f = mod3[:, 48:64, :]

    def convert(blocks):
        for b in blocks:
            P.dma('pool', wbf[b], wall[b], reads=[], writes=[('wbf', wgrp(b))], chan=wgrp(b))
    convert(list(range(NL * NBLK_L)))

    P.dma('sp', cmat, cmat_d, [], ['cmat'], 'ld_c')
    P.dma('sp', amask_f, amask_d, [], ['amask_f'], 'ld_c')
    P.dma('sp', cc, cc_d, [], ['cc'], 'ld_c')
    P.cp('dve', ident_b, ident_f, ['cmat'], ['cbf'])
    P.cp('dve', ones_b, cmat[:, 768:896], ['cmat'], ['cbf'])
    P.cp('dve', amask, amask_f, ['amask_f'], ['amask'])

    class Prefetch:
        def __init__(self, seq, depth=2, direct=False):
            self.direct = direct
            self.seq = seq
            self.i = 0
            self.issued = 0
            self.depth = depth
            self.slots = {}

        def _issue(self):
            b = self.seq[self.issued]
            slot = P.ring_i % P.NS
            P.ring_i += 1
            if self.direct:
                P.dma('pool', ring[slot], wall[b], reads=[], writes=[('ring', slot)], chan=f"ring{slot}")
            else:
                P.dma('sp', ring[slot], wbf[b], reads=[('wbf', wgrp(b))], writes=[('ring', slot)],
                      chan=f"ring{slot}")
            self.slots[self.issued] = slot
            self.issued += 1

        def get(self):
            while self.issued < len(self.seq) and self.issued <= self.i + self.depth:
                self._issue()
            slot = self.slots.pop(self.i)
            self.i += 1
            return ring[slot], ('ring', slot)

    def layer_setup(l):
        P.dma('sp', vecF, vecF_d[l], [], ['vecF'], 'ld_v')
        P.dma('sp', vecB, vecB_d[l].partition_broadcast(128), [], ['vecB'], 'ld_v')
        P.act(a_bc, vecB[:, 0:32], AF.Exp, ['vecB'], ['a_bc'])
        P.ts('dve', a_bc, a_bc, -1.0, None, ALU.mult, None, ['a_bc'], ['a_bc'])
        P.act(esink, vecB[:, 1104:1112], AF.Exp, ['vecB'], ['esink'])
        P.cp('dve', expsink.rearrange("p (h t) -> p h t", t=128),
             esink.unsqueeze(2).to_broadcast([128, 8, 128]), ['esink'], ['expsink'])

    def phase_mod(l):
        P.act(scb, cc, AF.Silu, ['cc'], ['scb'])
        scb3 = scb.rearrange("p (c j) -> p c j", j=2)
        pf = Prefetch([blk(l, OFF_WMOD, i) for i in range(24)])
        psM, kM = P.ps()
        for og in range(24):
            w, kw = pf.get()
            w3 = w.rearrange("p (k o) -> p k o", o=512)
            for oc in range(4):
                col = (og * 4 + oc) * 2
                for kc in range(KC):
                    P.mm(psM[:, col:col + 2], w3[:, kc, oc * 128:(oc + 1) * 128], scb3[:, kc, :],
                         kc == 0, kc == KC - 1, [kw, 'scb'], [kM])
        bmod = vecF[:, 136:232]
        P.tt('dve', mod3, psM[:, 0:192].rearrange("p (c j) -> p c j", j=2),
             bmod.unsqueeze(2).to_broadcast([128, 96, 2]), ALU.add, [kM, 'vecF'], ['mod'])

        def gb(i):
            return vecF[:, i * 16:(i + 1) * 16].unsqueeze(2).to_broadcast([128, 16, 2])
        P.stt('dve', A_m, mod3[:, 16:32, :], 1.0, gb(0), ALU.add, ALU.mult, ['mod', 'vecF'], ['der'])
        P.tt('dve', G_m, mod3[:, 32:48, :], gb(1), ALU.mult, ['mod', 'vecF'], ['der'])
        P.stt('dve', A_f, mod3[:, 64:80, :], 1.0, gb(2), ALU.add, ALU.mult, ['mod', 'vecF'], ['der'])
        P.tt('dve', G_f, mod3[:, 80:96, :], gb(3), ALU.mult, ['mod', 'vecF'], ['der'])

    def rms_rstd(x3, kx, sq3, ksq, rstd, krstd, T, nch=KC, dim=D):
        P.tt('pool', sq3[:, :, :T], x3, x3, ALU.mult, [kx], [ksq])
        pss, kps = P.ps()
        for c in range(nch):
            P.mm(pss[:, :T], ones_b, sq3[:, c, :T], c == 0, c == nch - 1, [ksq, 'cbf'], [kps])
        P.act(rstd[:, :T], pss[:, :T], AF.Sqrt, [kps], [krstd], bias=EPS, scale=1.0 / dim)
        P.recip(rstd[:, :T], rstd[:, :T], [krstd], [krstd])

    def phase_inproj(l):
        P.bar()
        P.aoff = persist_end
        P.ps_n = 7
        xsb = [P.alloc(8192)]
        xsb.append(xsb[0])
        xbcst = P.alloc(12 * 512)
        sqz = P.alloc(4096)
        sqn = P.alloc_bf(8192)
        hbuf = P.alloc_bf(8192)
        rstdb = [P.alloc(512), P.alloc(512)]
        tmp = [P.alloc(512) for _ in range(2)]
        tmp.append(tmp[0])
        qf = [P.alloc(512) for _ in range(2)]
        t1 = [P.alloc(512) for _ in range(2)]
        qst = P.alloc_bf(10 * 512)
        cs = [P.alloc(1024), P.alloc(1024)]
        vst = P.alloc_bf(4 * 256)
        dst = P.alloc(4 * 32)
        dtt = P.alloc(32)
        xsrc = xT0 if l == 0 else xTs
        seq = []
        for _ in tiles:
            seq += [blk(l, OFF_WINF, i) for i in range(6)] + [blk(l, OFF_WINT, i) for i in range(3)]
        pf = Prefetch(seq)
        tmp_i = [0]

        def tmp_next():
            i_ = tmp_i[0] % 2
            tmp_i[0] += 1
            return tmp[i_], f'tmp{i_}'

        def load_x(ti):
            t0, T, isctx = tiles[ti]
            xs = xsb[0].rearrange("p (c t) -> p c t", t=512)[:, :, :T]
            P.dma('sp', xs, xsrc[:, t0:t0 + T].rearrange("(c p) t -> p c t", p=128),
                  [('xT', None)] if l > 0 else [], ['xs0'], 'ld_xs')
            if not isctx:
                P.dma('sp', cs[ti % 2].rearrange("p (a t) -> p a t", t=512)[:, :, :T],
                      rope_d[:, :, t0 - LC:t0 - LC + T].rearrange("a p t -> p a t"), [], [f'cs{ti % 2}'],
                      f'ld_cs{ti % 2}')

        def stats(ti):
            t0, T, isctx = tiles[ti]
            xs = xsb[0].rearrange("p (c t) -> p c t", t=512)[:, :, :T]
            sq3 = sqn.rearrange("p (c t) -> p c t", t=512)
            pss, kps = P.bank(7)
            for c in range(KC):
                P.tt('pool', sq3[:, c, :T], xs[:, c, :], xs[:, c, :], ALU.mult, ['xs0'], [('sqn', c)])
            for c in range(KC):
                P.mm(pss[:, :T], ones_b, sq3[:, c, :T], c == 0, c == KC - 1, [('sqn', c), 'cbf'], [kps])
            r = rstdb[ti % 2]
            P.act(r[:, :T], pss[:, :T], AF.Sqrt, [kps], [f'rstd{ti % 2}'], bias=EPS, scale=1.0 / D)
            P.recip(r[:, :T], r[:, :T], [f'rstd{ti % 2}'], [f'rstd{ti % 2}'])

        load_x(0)
        stats(0)
        for ti, (t0, T, isctx) in enumerate(tiles):
            j = 1 if isctx else 0
            kx = 'xs0'
            xs = xsb[0].rearrange("p (c t) -> p c t", t=512)[:, :, :T]
            h3 = hbuf.rearrange("p (c t) -> p c t", t=512)
            rstd = rstdb[ti % 2]
            krs = f'rstd{ti % 2}'
            for c in range(KC):
                tm, ktm = tmp_next()
                P.stt('dve', tm[:, :T], xs[:, c, :], A_m[:, c, j:j + 1], rstd[:, :T], ALU.mult, ALU.mult,
                      [kx, 'der', krs], [ktm])
                P.act(h3[:, c, :T], tm[:, :T], AF.Identity, [ktm, 'mod'], [('h', c)], bias=B_m[:, c, j:j + 1],
                      scale=1.0)
            if ti + 1 < len(tiles):
                load_x(ti + 1)
            xb3 = xbcst.rearrange("p (c t) -> p c t", t=512)
            q3 = qst.rearrange("p (c t) -> p c t", t=512)
            cst = cs[ti % 2].rearrange("p (a t) -> p a t", t=512)
            for b in range(6):
                w, kw = pf.get()
                w3 = w.rearrange("p (k o) -> p k o", o=512)
                for oc in range(4):
                    g = b * 4 + oc
                    if g >= 22:
                        break
                    pso, kp = P.ps()
                    for kc in range(KC):
                        P.mm(pso[:, :T], w3[:, kc, oc * 128:(oc + 1) * 128], h3[:, kc, :T], kc == 0, kc == KC - 1,
                             [kw, ('h', kc)], [kp])
                    if g < 12:
                        if g % 2 == 0:
                            P.cp('act', xb3[:, g, :T], pso[:, :T], [kp], [('xbcst', g)])
                        else:
                            P.cp('dve', xb3[:, g, :T], pso[:, :T], [kp], [('xbcst', g)])
                    else:
                        hi = g - 12
                        if isctx:
                            P.cp('act', q3[:, hi, :T], pso[:, :T], [kp], [('qst', hi)])
                        else:
                            qq = qf[hi % 2]
                            kq = f'qf{hi % 2}'
                            tt1 = t1[hi % 2]
                            kt1 = f't1{hi % 2}'
                            P.cp('act', qq[:, :T], pso[:, :T], [kp], [kq])
                            ps2, kp2 = P.ps()
                            P.mm(ps2[:, :T], rot_f, qq[:, :T], True, True, [kq, 'cmat'], [kp2])
                            P.tt('pool', tt1[:, :T], qq[:, :T], cst[:, 0, :T], ALU.mult, [kq, f'cs{ti % 2}'], [kt1])
                            tm, ktm = tmp_next()
                            P.tt('dve', tm[:, :T], ps2[:, :T], cst[:, 1, :T], ALU.mult, [kp2, f'cs{ti % 2}'], [ktm])
                            P.tt('dve', q3[:, hi, :T], tm[:, :T], tt1[:, :T], ALU.add, [ktm, kt1], [('qst', hi)])
                if b == 1 and ti + 1 < len(tiles):
                    stats(ti + 1)
                if b == 2:
                    P.dma('sp', xbcT[:, t0:t0 + T].rearrange("(c p) t -> p c t", p=128), xb3[:, :, :T],
                          [('xbcst', None)], [('xbcT', ti)], 'st_xbc')
            P.dma('sp', qT_d[:, t0:t0 + T].rearrange("(c p) t -> p c t", p=128), q3[:, 0:8, :T],
                  [('qst', None)], [('qT', ti)], 'st_q')
            P.dma('sp', kT_d[:, t0:t0 + T].rearrange("(c p) t -> p c t", p=128), q3[:, 8:10, :T],
                  [('qst', None)], [('kT', ti)], 'st_q')
            nsub = T // 128
            zst = sqz.rearrange("p (s f) -> p s f", f=1024)
            vs3 = vst.rearrange("p (s f) -> p s f", f=256)
            ds3 = dst.rearrange("p (s f) -> p s f", f=32)
            for b in range(3):
                w, kw = pf.get()
                w3 = w.rearrange("p (k o) -> p k o", o=512)
                ncol = 512 if b < 2 else 288
                for sub in range(nsub):
                    pso, kp = P.ps()
                    for kc in range(KC):
                        P.mm(pso[:, :ncol], h3[:, kc, sub * 128:(sub + 1) * 128], w3[:, kc, 0:ncol], kc == 0,
                             kc == KC - 1, [kw, ('h', kc)], [kp])
                    if b < 2:
                        P.act(zst[:, sub, b * 512:(b + 1) * 512], pso[:, :], AF.Silu, [kp], [('sqz', ('z', sub, b))])
                    else:
                        P.cp('dve', vs3[:, sub, :], pso[:, 0:256], [kp], ['vst'])
                        P.tt('dve', dtt, pso[:, 256:288], vecB[:, 32:64], ALU.add, [kp, 'vecB'], ['dtt'])
                        P.act(dtt, dtt, AF.Exp, ['dtt'], ['dtt'])
                        P.act(ds3[:, sub, :], dtt, AF.Ln, ['dtt'], ['dst'], bias=1.0, scale=1.0)
            tsl = slice(t0, t0 + T)
            P.dma('sp', zs_d[tsl, :].rearrange("(s p) f -> p s f", p=128), zst[:, :nsub, :], [('sqz', None)],
                  [('zs', ti)], 'st_z')
            P.dma('sp', vtok_d[tsl, :].rearrange("(s p) f -> p s f", p=128), vs3[:, :nsub, :], ['vst'],
                  [('vtok', ti)], 'st_z')
            P.dma('sp', dts_d[tsl, :].rearrange("(s p) f -> p s f", p=128), ds3[:, :nsub, :], ['dst'],
                  [('dts', ti)], 'st_z')

    def phase_conv(l):
        P.bar()
        P.aoff = persist_end
        P.ps_n = 8
        ub = [P.alloc(12 * 516), P.alloc(12 * 516)]
        acc = [P.alloc(512) for _ in range(4)]
        xcs = P.alloc_bf(12 * 512)
        xtk = P.alloc_bf(4 * 1280)
        cw = vecF[:, 64:124].rearrange("p (c j) -> p c j", j=5)
        cb = vecF[:, 124:136]
        nt = len(tiles)

        def load_u(ti):
            t0, T, isctx = tiles[ti]
            u3 = ub[ti % 2].rearrange("p (c t) -> p c t", t=516)
            first = isctx or ti == 1
            last = isctx or ti == nt - 1
            lo = 0 if first else 2
            hi = 0 if last else 2
            ku = f'u{ti % 2}'
            if first:
                P.memset('pool', u3[:, :, 0:2], 0.0, [(ku, 'l')])
            if last:
                P.memset('pool', u3[:, :, T + 2:T + 4], 0.0, [(ku, 'r')])
            P.dma('sp', u3[:, :, 2 - lo:T + 2 + hi],
                  xbcT[:, t0 - lo:t0 + T + hi].rearrange("(c p) t -> p c t", p=128),
                  [('xbcT', None)], [(ku, 'm')], f'ld_u{ti % 2}')

        load_u(0)
        ai = 0
        for ti, (t0, T, isctx) in enumerate(tiles):
            if ti + 1 < nt:
                load_u(ti + 1)
            ku = f'u{ti % 2}'
            u3 = ub[ti % 2].rearrange("p (c t) -> p c t", t=516)
            x3 = xcs.rearrange("p (c t) -> p c t", t=512)
            for c in range(12):
                eng = 'dve'
                a = acc[ai % 4]
                ka = f'acc{ai % 4}'
                ai += 1
                P.ts(eng, a[:, :T], u3[:, c, 0:T], cw[:, c, 0:1], None, ALU.mult, None, [(ku, None), 'vecF'], [ka])
                for jj in range(1, 5):
                    P.stt(eng, a[:, :T], u3[:, c, jj:jj + T], cw[:, c, jj:jj + 1], a[:, :T], ALU.mult, ALU.add,
                          [(ku, None), 'vecF', ka], [ka])
                P.act(x3[:, c, :T], a[:, :T], AF.Silu, [ka, 'vecF'], [('xcs', c)], bias=cb[:, c:c + 1], scale=1.0)
            P.dma('sp', xcT_d[:, t0:t0 + T].rearrange("(c p) t -> p c t", p=128), x3[:, :, :T], [('xcs', None)],
                  [('xcT', ti)], 'st_xc')
            nsub = T // 128
            xt3 = xtk.rearrange("p (s f) -> p s f", f=1280)
            for sub in range(nsub):
                pA, kA = P.ps()
                pB, kB = P.ps()
                pAb = pA.bitcast(BF16)
                pBb = pB.bitcast(BF16)
                for c in range(8):
                    P.tr(pAb[:, c * 128:(c + 1) * 128], x3[:, c, sub * 128:(sub + 1) * 128], ident_b,
                         [('xcs', c), 'cbf'], [kA])
                for c in range(2):
                    P.tr(pBb[:, c * 128:(c + 1) * 128], x3[:, 8 + c, sub * 128:(sub + 1) * 128], ident_b,
                         [('xcs', 8 + c), 'cbf'], [kB])
                P.cp('dve', xt3[:, sub, 0:1024], pAb[:, 0:1024], [kA], [('xtk', sub)])
                P.cp('act', xt3[:, sub, 1024:1280], pBb[:, 0:256], [kB], [('xtk', sub)])
            P.dma('sp', xtok_d[t0:t0 + T, :].rearrange("(s p) f -> p s f", p=128), xt3[:, :nsub, :],
                  [('xtk', None)], [('xtok', ti)], 'st_xc')

    def phase_ssd(l, d):
        P.bar()
        P.aoff = persist_end
        NBUF = 2
        xtkb = [P.alloc_bf(1280) for _ in range(NBUF)]
        bctb = [P.alloc_bf(512) for _ in range(NBUF)]
        dtb = [P.alloc(32) for _ in range(NBUF)]
        zsb = [P.alloc(1024) for _ in range(NBUF)]
        yfb = [P.alloc(1024) for _ in range(NBUF)]
        hs = P.alloc(1024)
        hsb = P.alloc_bf(1024)
        ld = P.alloc(16)
        negacs = P.alloc(16)
        eacs = P.alloc(16)
        dtea = P.alloc(16)
        dte = P.alloc(16)
        cd = P.alloc(16)
        dec = [P.alloc(128) for _ in range(4)]
        MT = [P.alloc_bf(128) for _ in range(4)]
        xd = P.alloc_bf(1024)
        xdd = P.alloc_bf(1024)
        yacc = P.alloc(1024)
        ytmp = P.alloc(1024)
        ybf = P.alloc_bf(1024)
        yTs = P.alloc_bf(1024)
        ssq = P.alloc(2)
        junk = P.alloc(1024)
        P.memset('dve', hs, 0.0, ['hs'])
        P.memset('dve', hsb, 0.0, ['hsb'])
        if d == 0:
            order = [0, 1] + [2 + i for i in range(nlc)]
        else:
            order = [1, 0] + [2 + i for i in reversed(range(nlc))]
        llast = 127 if d == 0 else 0

        def load(i):
            cb_ = order[i]
            tc = cb_ * 128
            s = i % NBUF
            P.dma('sp', xtkb[s], xtok_d[tc:tc + 128, :], [('xtok', None)], [f'xtk{s}'], f'ld_a{s}')
            P.dma('sp', bctb[s].rearrange("p (c t) -> p c t", t=128),
                  xcT_d[1024:1536, tc:tc + 128].rearrange("(c p) t -> p c t", p=128), [('xcT', None)],
                  [f'bct{s}'], f'ld_a{s}')
            P.dma('sp', dtb[s], dts_d[tc:tc + 128, :], [('dts', None)], [f'dt{s}'], f'ld_a{s}')
            if d == 1:
                P.dma('sp', zsb[s], zs_d[tc:tc + 128, :], [('zs', None)], [f'zs{s}'], f'ld_b{s}')
                P.dma('sp', yfb[s], yf_d[tc:tc + 128, :], [('yf', None)], [f'yf{s}'], f'ld_b{s}')

        load(0)
        di = 0
        for i, cb_ in enumerate(order):
            if i + 1 < len(order):
                load(i + 1)
            s = i % NBUF
            tc = cb_ * 128
            xt = xtkb[s]
            kxt = f'xtk{s}'
            bct = bctb[s].rearrange("p (c t) -> p c t", t=128)
            kbct = f'bct{s}'
            dt_d = dtb[s][:, 16 * d:16 * d + 16]
            kdt = f'dt{s}'
            P.tt('dve', ld, dt_d, a_bc[:, 16 * d:16 * d + 16], ALU.mult, [kdt, 'a_bc'], ['ld'])
            ps0, k0 = P.bank(0)
            psm = ps0[:, 256:272]
            P.mm(psm, tri[d], ld, True, True, ['cmat', 'ld'], [k0])
            for g in range(2):
                P.mm(ps0[:, g * 128:(g + 1) * 128], bct[:, g, :], bct[:, 2 + g, :], True, True, [kbct], [k0])
            P.ts('dve', negacs, psm, -1.0, None, ALU.mult, None, [k0], ['negacs'])
            P.act(eacs, psm, AF.Exp, [k0], ['eacs'])
            P.tt('dve', xd.rearrange("p (h q) -> p h q", q=64), xt[:, 0:1024].rearrange("p (h q) -> p h q", q=64),
                 dt_d.unsqueeze(2).to_broadcast([128, 16, 64]), ALU.mult, [kxt, kdt], ['xd'])
            for g in range(2):
                po, ko = P.bank(5 + g)
                P.mm(po, bct[:, 2 + g, :], hsb[:, g * 512:(g + 1) * 512], True, True, [kbct, 'hsb'], [ko])
                sl = slice(g * 512, (g + 1) * 512)
                P.tt('dve', ytmp[:, sl].rearrange("p (h q) -> p h q", q=64), po.rearrange("p (h q) -> p h q", q=64),
                     eacs[:, g * 8:(g + 1) * 8].unsqueeze(2).to_broadcast([128, 8, 64]), ALU.mult, [ko, 'eacs'],
                     [('ytmp', g)])
            for r in range(2):
                pas = []
                for q in range(2):
                    pa, ka = P.bank(1 + q)
                    pas.append((pa, ka))
                    for hh in range(4):
                        h = r * 8 + q * 4 + hh
                        P.mm(pa[:, hh * 128:(hh + 1) * 128], ld[:, h:h + 1].to_broadcast([128, 128]), tri[d], True,
                             False, ['ld', 'cmat'], [ka])
                        P.mm(pa[:, hh * 128:(hh + 1) * 128], ident_f, mbias[d], False, True, ['cmat'], [ka])
                for q in range(2):
                    pa, ka = pas[q]
                    h0 = r * 8 + q * 4
                    lastcol = pa.rearrange("p (h t) -> p h t", t=128)[:, :, llast]
                    P.tt('dve', dtea[:, h0:h0 + 4], lastcol, negacs[:, h0:h0 + 4], ALU.add, [ka, 'negacs'],
                         [('dtea', h0)])
                    P.act(cd[:, h0:h0 + 4], lastcol, AF.Exp, [ka], [('cd', h0)])
                py, ky = P.bank(3 + r)
                for q in range(2):
                    pa, ka = pas[q]
                    for hh in range(4):
                        h = r * 8 + q * 4 + hh
                        dc = dec[di % 4]
                        kdc = f'dec{di % 4}'
                        m_ = MT[di % 4]
                        kmt = f'MT{di % 4}'
                        di += 1
                        P.act(dc, pa[:, hh * 128:(hh + 1) * 128], AF.Exp, [ka, 'negacs'], [kdc],
                              bias=negacs[:, h:h + 1], scale=1.0)
                        P.tt('dve', m_, ps0[:, r * 128:(r + 1) * 128], dc, ALU.mult, [k0, kdc], [kmt])
                        h8 = h % 8
                        P.mm(py[:, h8 * 64:(h8 + 1) * 64], m_, xd[:, h * 64:(h + 1) * 64], True, True, [kmt, 'xd'],
                             [ky])
                sl = slice(r * 512, (r + 1) * 512)
                P.tt('dve', yacc[:, sl], ytmp[:, sl], py, ALU.add, [('ytmp', r), ky], [('yacc', r)])
            P.act(dte, dtea, AF.Exp, [('dtea', None)], ['dte'])
            P.tt('dve', xdd.rearrange("p (h q) -> p h q", q=64), xd.rearrange("p (h q) -> p h q", q=64),
                 dte.unsqueeze(2).to_broadcast([128, 16, 64]), ALU.mult, ['xd', 'dte'], ['xdd'])
            P.tt('dve', hs.rearrange("p (h q) -> p h q", q=64), hs.rearrange("p (h q) -> p h q", q=64),
                 cd.unsqueeze(2).to_broadcast([128, 16, 64]), ALU.mult, ['hs', ('cd', None)], ['hs'])
            for g in range(2):
                pS, kS = P.bank(5 + g)
                P.mm(pS, xt[:, 1024 + g * 128:1024 + (g + 1) * 128], xdd[:, g * 512:(g + 1) * 512], True, True,
                     [kxt, 'xdd'], [kS])
                P.tt('dve', hs[:, g * 512:(g + 1) * 512], hs[:, g * 512:(g + 1) * 512], pS, ALU.add, ['hs', kS],
                     ['hs'])
            P.cp('act', hsb, hs, ['hs'], ['hsb'])
            if d == 0:
                P.dma('sp', yf_d[tc:tc + 128, :], yacc, [('yacc', None)], [('yf', cb_)], 'st_yf')
            else:
                P.tt('dve', yacc, yacc, yfb[s], ALU.add, [('yacc', None), f'yf{s}'], [('yacc', None)])
                P.tt('pool', ytmp.rearrange("p (h q) -> p h q", q=64), xt[:, 0:1024].rearrange("p (h q) -> p h q", q=64),
                     vecB[:, 64:80].unsqueeze(2).to_broadcast([128, 16, 64]), ALU.mult, [kxt, 'vecB'],
                     [('ytmp', None)])
                P.tt('dve', yacc, yacc, ytmp, ALU.add, [('yacc', None), ('ytmp', None)], [('yacc', None)])
                P.tt('dve', yacc, yacc, zsb[s], ALU.mult, [('yacc', None), f'zs{s}'], [('yacc', None)])
                P.act(junk, yacc, AF.Square, [('yacc', None)], ['junk', 'ssq'], accum_out=ssq[:, 0:1])
                P.act(ssq[:, 1:2], ssq[:, 0:1], AF.Sqrt, ['ssq'], ['ssq2'], bias=EPS, scale=1.0 / D_SSM)
                P.recip(ssq[:, 1:2], ssq[:, 1:2], ['ssq2'], ['ssq2'])
                P.stt('dve', ybf, yacc, ssq[:, 1:2], vecB[:, 80:1104], ALU.mult, ALU.mult,
                      [('yacc', None), 'ssq2', 'vecB'], ['ybf'])
                pT, kT_ = P.bank(7)
                pTb = pT.bitcast(BF16)
                for c in range(8):
                    P.tr(pTb[:, c * 128:(c + 1) * 128], ybf[:, c * 128:(c + 1) * 128], ident_b, ['ybf', 'cbf'], [kT_])
                P.cp('act', yTs, pTb[:, 0:1024], [kT_], ['yTs'])
                P.dma('sp', yT_d[0:1024, tc:tc + 128].rearrange("(c p) t -> p c t", p=128),
                      yTs.rearrange("p (c t) -> p c t", t=128), ['yTs'], [('yT', ('s', cb_))], 'st_yT')

    def phase_attn(l, last):
        P.bar()
        P.aoff = persist_end
        P.ps_n = 8
        kTg = P.alloc_bf(NT)
        Vg = P.alloc_bf(NT)
        q4b = [P.alloc_bf(2048), P.alloc_bf(2048)]
        pTb_ = [P.alloc_bf(512) for _ in range(6)]
        ost = [P.alloc_bf(2048), P.alloc_bf(2048)]
        den = [P.alloc(512), P.alloc(512)]
        scale = 1.0 / np.sqrt(128.0)
        qblocks = ([] if last else [0, 1]) + [2 + i for i in range(nlc)]
        groups = []
        if not last:
            groups.append([0, 1])
        for i in range(0, nlc, 4):
            groups.append([2 + i + k for k in range(4)])
        pi = 0
        for g in range(2):
            P.dma('sp', kTg, kT_d[g * 128:(g + 1) * 128, :], [('kT', None)], ['kTg'], 'ld_kv')
            V3 = Vg.rearrange("p (b d) -> p b d", d=128)
            P.dma('sp', V3, vtok_d[:, g * 128:(g + 1) * 128].rearrange("(b p) d -> p b d", p=128),
                  [('vtok', None)], ['Vg'], 'ld_kv')
            for gi, grp in enumerate(groups):
                s = gi % 2
                nq = len(grp)
                tq0 = grp[0] * 128
                q4 = q4b[s].rearrange("p (h t) -> p h t", t=512)
                P.dma('sp', q4[:, :, :nq * 128],
                      qT_d[g * 512:(g + 1) * 512, tq0:tq0 + nq * 128].rearrange("(h p) t -> p h t", p=128),
                      [('qT', None)], [f'q4{s}'], f'ld_q{s}')
                o4 = ost[s].rearrange("p (h t) -> p h t", t=512)
                for qi, qb in enumerate(grp):
                    if qb < 2:
                        keys = [(0, None), (1, None)]
                    else:
                        n = qb - 2
                        keys = []
                        if n > 0:
                            keys.append((qb - 1, 0))
                        keys.append((qb, None))
                        if n < nlc - 1:
                            keys.append((qb + 1, 1))
                        keys += [(0, None), (1, None)]
                    rq = q4[:, :, qi * 128:(qi + 1) * 128]
                    psO_, kO = P.ps()
                    psD, kD = P.ps()
                    for ki, (kb, mk) in enumerate(keys):
                        psS_, kS = P.ps()
                        P.mm(psS_.rearrange("p (h t) -> p h t", t=128), kTg[:, kb * 128:(kb + 1) * 128], rq, True,
                             mk is None, ['kTg', f'q4{s}'], [kS])
                        if mk is not None:
                            P.mm(psS_, ident_b, amask[:, mk * 512:(mk + 1) * 512], False, True, ['cbf', 'amask'], [kS])
                        pt = pTb_[pi % 6]
                        kpt = f'pT{pi % 6}'
                        pi += 1
                        P.act(pt, psS_, AF.Exp, [kS], [kpt], scale=float(scale))
                        P.mm(psO_, V3[:, kb, :], pt, ki == 0, ki == len(keys) - 1, ['Vg', kpt], [kO])
                        P.mm(psD, ones_b, pt, ki == 0, ki == len(keys) - 1, ['cbf', kpt], [kD])
                    dn = den[qi % 2]
                    kdn = f'den{qi % 2}'
                    P.tt('dve', dn, psD, expsink[:, g * 512:(g + 1) * 512], ALU.add, [kD, 'expsink'], [kdn])
                    P.recip(dn, dn, [kdn], [kdn])
                    P.tt('dve', o4[:, :, qi * 128:(qi + 1) * 128], psO_.rearrange("p (h t) -> p h t", t=128),
                         dn.rearrange("p (h t) -> p h t", t=128), ALU.mult, [kO, kdn], [(f'ost{s}', qi)])
                P.dma('sp', yT_d[1024 + g * 512:1024 + (g + 1) * 512, tq0:tq0 + nq * 128].rearrange(
                    "(h p) t -> p h t", p=128), o4[:, :, :nq * 128], [(f'ost{s}', None)], [('yT', ('a', g, gi))],
                    'st_yT')

    def phase_mlp(l, last):
        P.bar()
        P.aoff = persist_end
        P.ps_n = 7
        xb = P.alloc(8192)
        mixs = P.alloc(8192)
        bufA = P.alloc_bf(8192)
        ub = P.alloc_bf(32 * 512)
        ycat = P.alloc_bf(8192)
        rstd = P.alloc(512)
        tmp = [P.alloc(512) for _ in range(3)]
        rl = [P.alloc(512) for _ in range(2)]
        xsrc = xT0 if l == 0 else xTs
        my_tiles = [t for t in tiles if not (last and t[2])]
        seq = []
        for _ in my_tiles:
            seq += [blk(l, OFF_WOUT, i) for i in range(4)]
            for hf in range(2):
                seq += [blk(l, OFF_FF1, hf * 8 + i) for i in range(8)]
                seq += [blk(l, OFF_FF2, hf * 8 + i) for i in range(8)]
        pf = Prefetch(seq)
        cnt = {'tmp': 0, 'rl': 0}

        def nxt(lst, nm):
            i = cnt[nm] % len(lst)
            cnt[nm] += 1
            return lst[i], f'{nm}{i}'

        m3 = mixs.rearrange("p (c t) -> p c t", t=512)
        a3 = bufA.rearrange("p (c t) -> p c t", t=512)
        y3 = ycat.rearrange("p (c t) -> p c t", t=512)
        u3 = ub.rearrange("p (c t) -> p c t", t=512)

        def load_y(i):
            t0, T, isctx = my_tiles[i]
            P.dma('sp', y3[:, :, :T], yT_d[:, t0:t0 + T].rearrange("(c p) t -> p c t", p=128), [('yT', None)],
                  ['ycat'], 'ld_y')

        def finish_rstd(kps, pss, T):
            P.act(rstd[:, :T], pss[:, :T], AF.Sqrt, [kps], ['rstd'], bias=EPS, scale=1.0 / D)
            P.recip(rstd[:, :T], rstd[:, :T], ['rstd'], ['rstd'])

        load_y(0)
        for i_t, (t0, T, isctx) in enumerate(my_tiles):
            j = 1 if isctx else 0
            x3 = xb.rearrange("p (c t) -> p c t", t=512)[:, :, :T]
            P.dma('sp', x3, xsrc[:, t0:t0 + T].rearrange("(c p) t -> p c t", p=128),
                  [('xT', None)] if l > 0 else [], ['xb'], 'ld_xb')
            pss, kps = P.bank(7)
            pend = []

            def flush(last_=False):
                while pend:
                    c_ = pend.pop(0)
                    P.mm(pss[:, :T], ones_b, a3[:, c_, :T], c_ == 0, c_ == KC - 1, [('bufA', c_), 'cbf'], [kps])

            for b in range(4):
                w, kw = pf.get()
                w3 = w.rearrange("p (k o) -> p k o", o=512)
                for oc in range(4):
                    o = b * 4 + oc
                    pso, kp = P.ps()
                    for kc in range(KC):
                        P.mm(pso[:, :T], w3[:, kc, oc * 128:(oc + 1) * 128], y3[:, kc, :T], kc == 0, kc == KC - 1,
                             [kw, 'ycat'], [kp])
                    flush()
                    if o % 2 == 0:
                        P.cp('act', m3[:, o, :T], pso[:, :T], [kp], [('mixs', o)])
                    else:
                        P.cp('dve', m3[:, o, :T], pso[:, :T], [kp], [('mixs', o)])
                    P.tt('pool', a3[:, o, :T], m3[:, o, :T], m3[:, o, :T], ALU.mult, [('mixs', o)], [('bufA', o)])
                    pend.append(o)
            flush()
            finish_rstd(kps, pss, T)
            if i_t + 1 < len(my_tiles):
                load_y(i_t + 1)
            for c in range(KC):
                tm, ktm = nxt(tmp, 'tmp')
                P.tt('dve', tm[:, :T], m3[:, c, :T], rstd[:, :T], ALU.mult, [('mixs', c), 'rstd'], [ktm])
                P.stt('dve', x3[:, c, :], tm[:, :T], G_m[:, c, j:j + 1], x3[:, c, :], ALU.mult, ALU.add,
                      [ktm, 'der', ('xb', c)], [('xb', c)])
                P.tt('pool', a3[:, c, :T], x3[:, c, :], x3[:, c, :], ALU.mult, [('xb', c)], [('bufA', c)])
                P.mm(pss[:, :T], ones_b, a3[:, c, :T], c == 0, c == KC - 1, [('bufA', c), 'cbf'], [kps])
            finish_rstd(kps, pss, T)
            for c in range(KC):
                tm, ktm = nxt(tmp, 'tmp')
                P.stt('dve', tm[:, :T], x3[:, c, :], A_f[:, c, j:j + 1], rstd[:, :T], ALU.mult, ALU.mult,
                      [('xb', c), 'der', 'rstd'], [ktm])
                P.act(a3[:, c, :T], tm[:, :T], AF.Identity, [ktm, 'mod'], [('bufA', c)], bias=B_f[:, c, j:j + 1],
                      scale=1.0)
            for hf in range(2):
                for b in range(8):
                    w, kw = pf.get()
                    w3 = w.rearrange("p (k o) -> p k o", o=512)
                    for oc in range(4):
                        fcl = b * 4 + oc
                        pso, kp = P.ps()
                        for kc in range(KC):
                            P.mm(pso[:, :T], w3[:, kc, oc * 128:(oc + 1) * 128], a3[:, kc, :T], kc == 0, kc == KC - 1,
                                 [kw, ('bufA', kc)], [kp])
                        r, kr = nxt(rl, 'rl')
                        P.act(r[:, :T], pso[:, :T], AF.Relu, [kp], [kr])
                        P.tt('pool', u3[:, fcl, :T], r[:, :T], r[:, :T], ALU.mult, [kr], [('ub', fcl)])
                for b in range(8):
                    w, kw = pf.get()
                    w3 = w.rearrange("p (k o) -> p k o", o=256)
                    for oc in range(2):
                        o = b * 2 + oc
                        pso, kp = P.ps()
                        for fc in range(32):
                            P.mm(pso[:, :T], w3[:, fc, oc * 128:(oc + 1) * 128], u3[:, fc, :T], fc == 0, fc == 31,
                                 [kw, ('ub', fc)], [kp])
                        if hf == 0:
                            P.cp('act', m3[:, o, :T], pso[:, :T], [kp], [('mixs', o)])
                        else:
                            flush()
                            P.tt('dve', m3[:, o, :T], m3[:, o, :T], pso[:, :T], ALU.add, [('mixs', o), kp],
                                 [('mixs', o)])
                            P.tt('pool', a3[:, o, :T], m3[:, o, :T], m3[:, o, :T], ALU.mult, [('mixs', o)], [('bufA', o)])
                            pend.append(o)
            flush()
            finish_rstd(kps, pss, T)
            for c in range(KC):
                tm, ktm = nxt(tmp, 'tmp')
                P.tt('dve', tm[:, :T], m3[:, c, :T], rstd[:, :T], ALU.mult, [('mixs', c), 'rstd'], [ktm])
                P.stt('dve', x3[:, c, :], tm[:, :T], G_f[:, c, j:j + 1], x3[:, c, :], ALU.mult, ALU.add,
                      [ktm, 'der', ('xb', c)], [('xb', c)])
            if last:
                P.dma('sp', outT[:, t0 - LC:t0 - LC + T].rearrange("(c p) t -> p c t", p=128), x3, [('xb', None)],
                      [('outT', t0)], 'st_x')
            else:
                P.dma('sp', xTs[:, t0:t0 + T].rearrange("(c p) t -> p c t", p=128), x3, [('xb', None)],
                      [('xT', t0)], 'st_x')

    def convert_rest():
        rest = [blk(0, o, 0) + i for o, n in ((OFF_WOUT, 4), (OFF_FF1, 16), (OFF_FF2, 16)) for i in range(n)]
        for l_ in range(1, NL):
            rest += [blk(l_, OFF_WINF, i) for i in range(NBLK_L - OFF_WINF)]
        convert(rest)

    phases = []
    for l in range(NL):
        last = (l == NL - 1)
        phases += [lambda l=l: layer_setup(l), lambda l=l: phase_mod(l),
                   lambda l=l: phase_inproj(l),
                   lambda l=l: phase_conv(l), lambda l=l: phase_ssd(l, 0), lambda l=l: phase_ssd(l, 1),
                   lambda l=l, last=last: phase_attn(l, last), lambda l=l, last=last: phase_mlp(l, last)]
    if stop_after is not None:
        phases = phases[:stop_after]
    for ph in phases:
        ph()
    P.bar()
    P.op('sp', lambda e: e.nop(), (), ())
    P.S.emit()
    return nc


def _pack_k2048(w):
    K, C = w.shape
    nb = C // 512
    return np.ascontiguousarray(w.reshape(16, 128, nb, 512).transpose(2, 1, 0, 3)).reshape(nb, 128, 8192)


def _fm(v, nch):
    return np.ascontiguousarray(v.reshape(nch, 128).T)


def pack_weights(inp, NL):
    wall = np.zeros((NL * NBLK_L, 128, 8192), np.float32)
    vecF = np.zeros((NL, 128, NVF), np.float32)
    vecB = np.zeros((NL, 1, NVB), np.float32)
    for l in range(NL):
        base = l * NBLK_L
        wall[base + OFF_WMOD:base + OFF_WMOD + 24] = _pack_k2048(inp['w_mod'][l])
        wi = inp['w_in'][l]
        wF = np.zeros((2048, 3072), np.float32)
        wF[:, 0:1536] = wi[:, 0:1536]
        wF[:, 1536:2560] = wi[:, 2592:3616]
        wF[:, 2560:2816] = wi[:, 3616:3872]
        wall[base + OFF_WINF:base + OFF_WINF + 6] = _pack_k2048(wF)
        wT = np.zeros((2048, 1536), np.float32)
        wT[:, 0:1024] = wi[:, 1536:2560]
        wT[:, 1024:1280] = wi[:, 3872:4128]
        wT[:, 1280:1312] = wi[:, 2560:2592]
        wall[base + OFF_WINT:base + OFF_WINT + 3] = _pack_k2048(wT)
        wall[base + OFF_WOUT:base + OFF_WOUT + 4] = _pack_k2048(inp['w_out'][l])
        wall[base + OFF_FF1:base + OFF_FF1 + 16] = _pack_k2048(inp['w_ff1'][l])
        w2 = inp['w_ff2'][l]
        wall[base + OFF_FF2:base + OFF_FF2 + 16] = np.ascontiguousarray(
            w2.reshape(2, 32, 128, 8, 256).transpose(0, 3, 2, 1, 4)).reshape(16, 128, 8192)
        vecF[l, :, 0:16] = _fm(inp['g_pre_mix'][l], 16)
        vecF[l, :, 16:32] = _fm(inp['g_post_mix'][l], 16)
        vecF[l, :, 32:48] = _fm(inp['g_pre_mlp'][l], 16)
        vecF[l, :, 48:64] = _fm(inp['g_post_mlp'][l], 16)
        cw = inp['conv_w'][l]
        vecF[l, :, 64:124] = np.ascontiguousarray(cw.T.reshape(12, 128, 5).transpose(1, 0, 2)).reshape(128, 60)
        vecF[l, :, 124:136] = _fm(inp['conv_b'][l], 12)
        vecF[l, :, 136:232] = _fm(inp['b_mod'][l], 96)
        vecB[l, 0, 0:32] = inp['a_log'][l].reshape(32)
        vecB[l, 0, 32:64] = inp['dt_bias'][l].reshape(32)
        vecB[l, 0, 64:80] = inp['d_skip'][l]
        vecB[l, 0, 80:1104] = inp['ssm_norm'][l]
        vecB[l, 0, 1104:1112] = inp['attn_sink'][l]
    return wall, vecF, vecB


def make_consts(L):
    cm = np.zeros((128, 7 * 128), np.float32)
    i = np.arange(128)
    cm[:, 0:128] = np.eye(128)
    cm[:, 128:256] = (i[:, None] <= i[None, :])
    cm[:, 256:384] = (i[:, None] >= i[None, :])
    cm[:, 384:512] = np.where(i[:, None] <= i[None, :], 0.0, -BIG)
    cm[:, 512:640] = np.where(i[:, None] >= i[None, :], 0.0, -BIG)
    partner = np.where((i // 32) % 2 == 0, i + 32, i - 32)
    rot = np.zeros((128, 128), np.float32)
    rot[partner, i] = 1.0
    cm[:, 640:768] = rot
    cm[:, 768:896] = 1.0
    am = np.zeros((128, 1024), np.float32)
    prev = np.where(i[:, None] >= i[None, :], 0.0, -BIG)
    nxt = np.where(i[:, None] <= i[None, :], 0.0, -BIG)
    am[:, 0:512] = np.tile(prev, (1, 4))
    am[:, 512:1024] = np.tile(nxt, (1, 4))
    nf = 32
    inv = (10000.0 ** (-np.arange(nf, dtype=np.float32) / nf)).astype(np.float32)
    t = np.arange(L)
    pos_row = (t // 64).astype(np.float32)
    pos_col = (t % 64).astype(np.float32)
    ang_r = pos_row[None, :] * inv[:, None]
    ang_c = pos_col[None, :] * inv[:, None]
    cos = np.concatenate([np.cos(ang_r), np.cos(ang_r), np.cos(ang_c), np.cos(ang_c)], 0)
    sin = np.concatenate([-np.sin(ang_r), np.sin(ang_r), -np.sin(ang_c), np.sin(ang_c)], 0)
    rope = np.stack([cos, sin]).astype(np.float32)
    return cm, am, rope


def make_in_map(inp, b, L, shared):
    wall, vecF, vecB, cm, am, rope = shared
    xT = np.ascontiguousarray(np.concatenate([inp['ctx'][b], inp['x'][b][:L]], 0).T)
    ccv = np.stack([_fm(inp['c'][b], 16), _fm(inp['c_ctx'], 16)], -1).reshape(128, 32)
    return {"xT0": xT, "cc": np.ascontiguousarray(ccv), "wall": wall, "vecF": vecF, "vecB": vecB, "cmat": cm,
            "amask": am, "rope": rope}


def kernel(**inputs):
    inp = {k: np.asarray(v, dtype=np.float32) for k, v in inputs.items()}
    L = inp['x'].shape[1]
    NL = inp['w_in'].shape[0]
    B = inp['x'].shape[0]
    nc = build_program(L, NL)
    shared = pack_weights(inp, NL) + make_consts(L)
    in_maps = [make_in_map(inp, c % B, L, shared) for c in range(8)]
    res = run_bass_kernel_spmd(nc, in_maps, core_ids=list(range(8)))
    out = np.stack([np.ascontiguousarray(res.results[b]["outT"].T) for b in range(B)], 0)
    return out.astype(np.float32)
```

```python
import numpy as np
import concourse.bass as bass
import concourse.mybir as mybir
from concourse.bass_utils import run_bass_kernel_spmd

F32 = mybir.dt.float32
BF16 = mybir.dt.bfloat16
AF = mybir.ActivationFunctionType
ALU = mybir.AluOpType

D = 2048
KC = 16
LC = 256
D_SSM = 1024
D_XBC = 1536
NH = 16
HP = 64
NST = 128
D_FF = 8192
EPS = 1e-6
BIG = 30000.0
NBLK_L = 69
OFF_WMOD, OFF_WINF, OFF_WINT, OFF_WOUT, OFF_FF1, OFF_FF2 = 0, 24, 30, 33, 37, 53
NVF = 232
NVB = 1112
ENG = ['pe', 'act', 'dve', 'pool', 'sp']


class Sched:
    def __init__(self, nc):
        self.nc = nc
        self.ops = {e: [] for e in ENG}
        self.lastw = {}
        self.readers = {}
        self.seen = {e: {} for e in ENG}
        self.seend = {e: {} for e in ENG}
        self.chan_cnt = {}

    @staticmethod
    def _norm(b):
        return b if isinstance(b, tuple) else (b, None)

    def _deps(self, reads, writes):
        deps = []
        for (b, k) in reads:
            lw = self.lastw.get(b, {})
            toks = list(lw.values()) if k is None else [lw[x] for x in (k, None) if x in lw]
            deps += [(t, 'raw') for t in toks]
        for (b, k) in writes:
            lw = self.lastw.get(b, {})
            rd = self.readers.get(b, {})
            if k is None:
                deps += [(t, 'waw') for t in lw.values()]
                for d_ in rd.values():
                    deps += [(t, 'war') for t in d_.values()]
            else:
                for x in (k, None):
                    if x in lw:
                        deps.append((lw[x], 'waw'))
                    if x in rd:
                        deps += [(t, 'war') for t in rd[x].values()]
        return deps

    def _commit(self, tok, reads, writes):
        for (b, k) in reads:
            d_ = self.readers.setdefault(b, {}).setdefault(k, {})
            d_[(tok[0], tok[1])] = tok
        for (b, k) in writes:
            if k is None:
                self.lastw[b] = {None: tok}
                self.readers[b] = {}
            else:
                self.lastw.setdefault(b, {})[k] = tok
                self.readers.setdefault(b, {})[k] = {}

    def op(self, eng, fn, reads=(), writes=(), chan=None):
        reads = [self._norm(r) for r in reads]
        writes = [self._norm(w) for w in writes]
        deps = self._deps(reads, writes)
        idx = len(self.ops[eng])
        waits = []
        for (tok, kind) in deps:
            if tok[0] == 'e':
                _, A, i = tok
                if A == eng and (eng == 'pe' or kind != 'raw'):
                    continue
                if self.seen[eng].get(A, -1) >= i:
                    continue
                self.seen[eng][A] = i
                waits.append(tok)
                self.ops[A][i]['sig'] = True
            else:
                _, ch, cnt = tok
                cnt = self.chan_cnt[ch]
                if self.seend[eng].get(ch, 0) >= cnt:
                    continue
                self.seend[eng][ch] = cnt
                waits.append(('d', ch, cnt))
        self.ops[eng].append(dict(fn=fn, waits=waits, sig=False, chan=chan))
        if chan is None:
            tok = ('e', eng, idx)
        else:
            self.chan_cnt[chan] = self.chan_cnt.get(chan, 0) + 16
            tok = ('d', chan, self.chan_cnt[chan])
        self._commit(tok, reads, writes)

    def barrier(self):
        for e in ENG:
            for A in ENG:
                if A == e or not self.ops[A]:
                    continue
                i = len(self.ops[A]) - 1
                while i >= 0 and self.ops[A][i]['chan'] is not None:
                    i -= 1
                if i < 0 or self.seen[e].get(A, -1) >= i:
                    continue
                self.seen[e][A] = i
                self.ops[A][i]['sig'] = True
                self._pending_bar.setdefault(e, []).append(('e', A, i))
            for ch, cnt in self.chan_cnt.items():
                if ch.startswith('cv') or self.seend[e].get(ch, 0) >= cnt:
                    continue
                self.seend[e][ch] = cnt
                self._pending_bar.setdefault(e, []).append(('d', ch, cnt))

    _pending_bar = None

    def emit(self):
        nc = self.nc
        sems = {e: nc.alloc_semaphore(f"s_{e}") for e in ENG}
        chsem = {ch: nc.alloc_semaphore(f"c_{ch}") for ch in self.chan_cnt}
        for e in ENG:
            c = 0
            for o in self.ops[e]:
                if o['chan'] is None and o['sig']:
                    c += 1
                    o['sv'] = c
        ops = self.ops
        chan_cnt = self.chan_cnt

        def run(e, eng):
            for o in ops[e]:
                best = {}
                for w in o['waits']:
                    if w[0] == 'e':
                        key = ('e', w[1])
                        val = ops[w[1]][w[2]]['sv']
                    else:
                        key = ('d', w[1])
                        val = w[2]
                    best[key] = max(best.get(key, 0), val)
                for key, val in best.items():
                    sem = sems[key[1]] if key[0] == 'e' else chsem[key[1]]
                    eng.wait_ge(sem, val)
                ins = o['fn'](eng)
                if o['chan'] is not None:
                    ins.then_inc(chsem[o['chan']], 16)
                elif o['sig']:
                    ins.then_inc(sems[e], 1)
            if e == 'sp':
                for ch, cnt in chan_cnt.items():
                    eng.wait_ge(chsem[ch], cnt)

        with nc.Block() as block:
            block.tensor(lambda t: run('pe', t))
            block.scalar(lambda t: run('act', t))
            block.vector(lambda t: run('dve', t))
            block.gpsimd(lambda t: run('pool', t))
            block.sync(lambda t: run('sp', t))


class Prog:
    def __init__(self, L, NL, dbg=()):
        self.L = L
        self.NL = NL
        self.NT = LC + L
        self.dbg = set(dbg)
        self.nc = bass.Bass("TRN2", target_bir_lowering=False)
        self.S = Sched(self.nc)
        self.S._pending_bar = {}
        nc = self.nc
        self.arena = nc.alloc_sbuf_tensor("arena", [128, 53000], F32)
        self.aoff = 0
        self.psum = nc.alloc_psum_tensor("psum", [128, 4096], F32)
        self.ps_i = 0
        self.ring_i = 0
        self.NS = 3

    def alloc(self, n_f32):
        o = self.aoff
        self.aoff += (n_f32 + 7) // 8 * 8
        assert self.aoff <= 53000, self.aoff
        return self.arena[:, o:o + n_f32]

    def alloc_bf(self, n_bf16):
        return self.alloc((n_bf16 + 1) // 2).bitcast(BF16)

    ps_n = 8

    def ps(self):
        b = self.ps_i % self.ps_n
        self.ps_i += 1
        return self.psum[:, b * 512:(b + 1) * 512], ('ps', b)

    def bank(self, b):
        return self.psum[:, b * 512:(b + 1) * 512], ('ps', b)

    def bar(self):
        S = self.S
        S._pending_bar = {}
        S.barrier()
        pend = S._pending_bar
        for e, lst in pend.items():
            self._prewaits.setdefault(e, []).extend(lst)

    _prewaits = None

    def op(self, eng, fn, reads=(), writes=(), chan=None):
        self.S.op(eng, fn, reads, writes, chan)
        if self._prewaits and eng in self._prewaits:
            self.S.ops[eng][-1]['waits'].extend(self._prewaits.pop(eng))

    def mm(self, out, lhsT, rhs, start, stop, reads, writes):
        self.op('pe', lambda e: e.matmul(out, lhsT, rhs, start=start, stop=stop), reads, writes)

    def tr(self, out, in_, ident, reads, writes):
        self.op('pe', lambda e: e.transpose(out, in_, ident), reads, writes)

    def act(self, out, in_, func, reads, writes, bias=None, scale=None, accum_out=None, eng='act'):
        kw = {}
        if bias is not None:
            kw['bias'] = bias
        if scale is not None:
            kw['scale'] = scale
        if accum_out is not None:
            kw['accum_out'] = accum_out
        self.op(eng, lambda e: e.activation(out=out, in_=in_, func=func, **kw), reads, writes)

    def tt(self, eng, out, in0, in1, op, reads, writes):
        self.op(eng, lambda e: e.tensor_tensor(out=out, in0=in0, in1=in1, op=op), reads, writes)

    def ts(self, eng, out, in0, s1, s2, op0, op1, reads, writes):
        if s2 is None:
            self.op(eng, lambda e: e.tensor_scalar(out=out, in0=in0, scalar1=s1, scalar2=None, op0=op0),
                    reads, writes)
        else:
            self.op(eng, lambda e: e.tensor_scalar(out=out, in0=in0, scalar1=s1, scalar2=s2, op0=op0, op1=op1),
                    reads, writes)

    def stt(self, eng, out, in0, scalar, in1, op0, op1, reads, writes):
        self.op(eng, lambda e: e.scalar_tensor_tensor(out=out, in0=in0, scalar=scalar, in1=in1, op0=op0, op1=op1),
                reads, writes)

    def cp(self, eng, out, in_, reads, writes):
        if eng == 'act':
            self.op(eng, lambda e: e.copy(out=out, in_=in_), reads, writes)
        else:
            self.op(eng, lambda e: e.tensor_copy(out=out, in_=in_), reads, writes)

    def recip(self, out, in_, reads, writes):
        self.op('dve', lambda e: e.reciprocal(out=out, in_=in_), reads, writes)

    def memset(self, eng, ap, val, writes):
        self.op(eng, lambda e: e.memset(ap, val), (), writes)

    def dma(self, eng, out, in_, reads, writes, chan):
        self.op(eng, lambda e: e.dma_start(out=out, in_=in_), reads, writes, chan)


def blk(l, off, i):
    return l * NBLK_L + off + i


WGRP = [(0, 8, 'm0'), (8, 16, 'm1'), (16, 24, 'm2'), (24, 33, 'in'), (33, 37, 'wo'),
        (37, 45, 'f1a'), (45, 53, 'f1b'), (53, 61, 'f2a'), (61, 69, 'f2b')]


def wgrp(b):
    l, r = divmod(b, NBLK_L)
    for lo, hi, nm in WGRP:
        if lo <= r < hi:
            return f"cv{l}{nm}"
    raise ValueError


def build_program(L=4096, NL=2, dbg=(), stop_after=None):
    P = Prog(L, NL, dbg)
    P._prewaits = {}
    nc = P.nc
    NT = P.NT
    NB = NT // 128
    nlat_tiles = L // 512
    tiles = [(0, 256, True)] + [(LC + i * 512, 512, False) for i in range(nlat_tiles)]
    nlc = L // 128

    def dram(name, shape, dt, kind="Internal"):
        if name in P.dbg:
            kind = "ExternalOutput"
        return nc.dram_tensor(name, shape, dt, kind=kind).ap()

    xT0 = dram("xT0", [D, NT], F32, "ExternalInput")
    cc_d = dram("cc", [128, KC * 2], F32, "ExternalInput")
    wall = dram("wall", [NL * NBLK_L, 128, 8192], F32, "ExternalInput")
    vecF_d = dram("vecF", [NL, 128, NVF], F32, "ExternalInput")
    vecB_d = dram("vecB", [NL, 1, NVB], F32, "ExternalInput")
    cmat_d = dram("cmat", [128, 7 * 128], F32, "ExternalInput")
    amask_d = dram("amask", [128, 2 * 512], F32, "ExternalInput")
    rope_d = dram("rope", [2, 128, L], F32, "ExternalInput")
    outT = dram("outT", [D, L], F32, "ExternalOutput")

    wbf_l = [dram(f"wbf{l_}", [NBLK_L, 128, 8192], BF16) for l_ in range(NL)]

    class _WB:
        def __getitem__(self, b):
            return wbf_l[b // NBLK_L][b % NBLK_L]
    wbf = _WB()
    xTs = dram("xTs", [D, NT], F32)
    xbcT = dram("xbcT", [D_XBC, NT], F32)
    zs_d = dram("zs", [NT, 1024], F32)
    dts_d = dram("dts", [NT, 32], F32)
    qT_d = dram("qT", [1024, NT], BF16)
    kT_d = dram("kT", [256, NT], BF16)
    vtok_d = dram("vtok", [NT, 256], BF16)
    xcT_d = dram("xcT", [D_XBC, NT], BF16)
    xtok_d = dram("xtok", [NT, 1280], BF16)
    yf_d = dram("yf", [NT, 1024], F32)
    yT_d = dram("yT", [D, NT], BF16)

    ring = [P.alloc_bf(8192) for _ in range(P.NS)]
    cmat = P.alloc(7 * 128)
    ident_f = cmat[:, 0:128]
    tri = [cmat[:, 128:256], cmat[:, 256:384]]
    mbias = [cmat[:, 384:512], cmat[:, 512:640]]
    rot_f = cmat[:, 640:768]
    cbf = P.alloc_bf(2 * 128)
    ident_b = cbf[:, 0:128]
    ones_b = cbf[:, 128:256]
    amask = P.alloc_bf(1024)
    cc = P.alloc(32)
    scb = P.alloc_bf(32)
    vecF = P.alloc(NVF)
    vecB = P.alloc(NVB)
    mod = P.alloc(192)
    der = P.alloc(4 * 32)
    a_bc = P.alloc(32)
    esink = P.alloc(8)
    expsink = P.alloc(1024)
    persist_end = P.aoff
    amask_f = P.alloc(1024)

    mod3 = mod.rearrange("p (c j) -> p c j", j=2)
    A_m = der[:, 0:32].rearrange("p (c j) -> p c j", j=2)
    G_m = der[:, 32:64].rearrange("p (c j) -> p c j", j=2)
    A_f = der[:, 64:96].rearrange("p (c j) -> p c j", j=2)
    G_f = der[:, 96:128].rearrange("p (c j) -> p c j", j=2)
    B_m = mod3[:, 0:16, :]
    B_f = mod3[:, 48:64, :]

    def convert(blocks):
        for b in blocks:
            P.dma('pool', wbf[b], wall[b], reads=[], writes=[('wbf', wgrp(b))], chan=wgrp(b))

    P.dma('sp', cmat, cmat_d, [], ['cmat'], 'ld_c')
    P.dma('sp', amask_f, amask_d, [], ['amask_f'], 'ld_c')
    P.dma('sp', cc, cc_d, [], ['cc'], 'ld_c')
    P.cp('dve', ident_b, ident_f, ['cmat'], ['cbf'])
    P.cp('dve', ones_b, cmat[:, 768:896], ['cmat'], ['cbf'])
    P.cp('dve', amask, amask_f, ['amask_f'], ['amask'])

    class Prefetch:
        def __init__(self, seq, depth=2, direct=False):
            self.direct = direct
            self.seq = seq
            self.i = 0
            self.issued = 0
            self.depth = depth
            self.slots = {}

        def _issue(self):
            b = self.seq[self.issued]
            slot = P.ring_i % P.NS
            P.ring_i += 1
            if self.direct:
                P.dma('pool', ring[slot], wall[b], reads=[], writes=[('ring', slot)], chan=f"ringd{slot}")
            else:
                P.dma('sp', ring[slot], wbf[b], reads=[('wbf', wgrp(b))], writes=[('ring', slot)],
                      chan=f"ring{slot}")
            self.slots[self.issued] = slot
            self.issued += 1

        def get(self):
            while self.issued < len(self.seq) and self.issued <= self.i + self.depth:
                self._issue()
            slot = self.slots.pop(self.i)
            self.i += 1
            return ring[slot], ('ring', slot)

    def layer_setup(l):
        P.dma('sp', vecF, vecF_d[l], [], ['vecF'], 'ld_v')
        P.dma('sp', vecB, vecB_d[l].partition_broadcast(128), [], ['vecB'], 'ld_v')
        P.act(a_bc, vecB[:, 0:32], AF.Exp, ['vecB'], ['a_bc'])
        P.ts('dve', a_bc, a_bc, -1.0, None, ALU.mult, None, ['a_bc'], ['a_bc'])
        P.act(esink, vecB[:, 1104:1112], AF.Exp, ['vecB'], ['esink'])
        P.cp('dve', expsink.rearrange("p (h t) -> p h t", t=128),
             esink.unsqueeze(2).to_broadcast([128, 8, 128]), ['esink'], ['expsink'])

    def phase_mod(l):
        P.act(scb, cc, AF.Silu, ['cc'], ['scb'])
        scb3 = scb.rearrange("p (c j) -> p c j", j=2)
        pf = Prefetch([blk(l, OFF_WMOD, i) for i in range(24)], direct=True)
        psM, kM = P.ps()
        for og in range(24):
            w, kw = pf.get()
            if og == 0 and l == 0:
                convert([blk(0, OFF_WINF, i) for i in range(9)])
            w3 = w.rearrange("p (k o) -> p k o", o=512)
            for oc in range(4):
                col = (og * 4 + oc) * 2
                for kc in range(KC):
                    P.mm(psM[:, col:col + 2], w3[:, kc, oc * 128:(oc + 1) * 128], scb3[:, kc, :],
                         kc == 0, kc == KC - 1, [kw, 'scb'], [kM])
        bmod = vecF[:, 136:232]
        P.tt('dve', mod3, psM[:, 0:192].rearrange("p (c j) -> p c j", j=2),
             bmod.unsqueeze(2).to_broadcast([128, 96, 2]), ALU.add, [kM, 'vecF'], ['mod'])

        def gb(i):
            return vecF[:, i * 16:(i + 1) * 16].unsqueeze(2).to_broadcast([128, 16, 2])
        P.stt('dve', A_m, mod3[:, 16:32, :], 1.0, gb(0), ALU.add, ALU.mult, ['mod', 'vecF'], ['der'])
        P.tt('dve', G_m, mod3[:, 32:48, :], gb(1), ALU.mult, ['mod', 'vecF'], ['der'])
        P.stt('dve', A_f, mod3[:, 64:80, :], 1.0, gb(2), ALU.add, ALU.mult, ['mod', 'vecF'], ['der'])
        P.tt('dve', G_f, mod3[:, 80:96, :], gb(3), ALU.mult, ['mod', 'vecF'], ['der'])

    def rms_rstd(x3, kx, sq3, ksq, rstd, krstd, T, nch=KC, dim=D):
        P.tt('pool', sq3[:, :, :T], x3, x3, ALU.mult, [kx], [ksq])
        pss, kps = P.ps()
        for c in range(nch):
            P.mm(pss[:, :T], ones_b, sq3[:, c, :T], c == 0, c == nch - 1, [ksq, 'cbf'], [kps])
        P.act(rstd[:, :T], pss[:, :T], AF.Sqrt, [kps], [krstd], bias=EPS, scale=1.0 / dim)
        P.recip(rstd[:, :T], rstd[:, :T], [krstd], [krstd])

    def phase_inproj(l):
        P.bar()
        P.aoff = persist_end
        P.ps_n = 7
        xsb = [P.alloc(8192)]
        xsb.append(xsb[0])
        xbcst = P.alloc(12 * 512)
        sqz = P.alloc(4096)
        sqn = P.alloc_bf(8192)
        hbuf = P.alloc_bf(8192)
        rstdb = [P.alloc(512), P.alloc(512)]
        tmp = [P.alloc(512) for _ in range(2)]
        tmp.append(tmp[0])
        qf = [P.alloc(512) for _ in range(2)]
        t1 = [P.alloc(512) for _ in range(2)]
        qst = P.alloc_bf(10 * 512)
        cs = [P.alloc(1024), P.alloc(1024)]
        vst = P.alloc_bf(4 * 256)
        dst = P.alloc(4 * 32)
        dtt = P.alloc(32)
        xsrc = xT0 if l == 0 else xTs
        seq = []
        for _ in tiles:
            seq += [blk(l, OFF_WINF, i) for i in range(6)] + [blk(l, OFF_WINT, i) for i in range(3)]
        pf = Prefetch(seq)
        tmp_i = [0]

        def tmp_next():
            i_ = tmp_i[0] % 2
            tmp_i[0] += 1
            return tmp[i_], f'tmp{i_}'

        def load_x(ti):
            t0, T, isctx = tiles[ti]
            xs = xsb[0].rearrange("p (c t) -> p c t", t=512)[:, :, :T]
            P.dma('sp', xs, xsrc[:, t0:t0 + T].rearrange("(c p) t -> p c t", p=128),
                  [('xT', None)] if l > 0 else [], ['xs0'], 'ld_xs')
            if not isctx:
                P.dma('sp', cs[ti % 2].rearrange("p (a t) -> p a t", t=512)[:, :, :T],
                      rope_d[:, :, t0 - LC:t0 - LC + T].rearrange("a p t -> p a t"), [], [f'cs{ti % 2}'],
                      f'ld_cs{ti % 2}')

        def stats(ti):
            t0, T, isctx = tiles[ti]
            xs = xsb[0].rearrange("p (c t) -> p c t", t=512)[:, :, :T]
            sq3 = sqn.rearrange("p (c t) -> p c t", t=512)
            pss, kps = P.bank(7)
            for c in range(KC):
                P.tt('pool', sq3[:, c, :T], xs[:, c, :], xs[:, c, :], ALU.mult, ['xs0'], [('sqn', c)])
            for c in range(KC):
                P.mm(pss[:, :T], ones_b, sq3[:, c, :T], c == 0, c == KC - 1, [('sqn', c), 'cbf'], [kps])
            r = rstdb[ti % 2]
            P.act(r[:, :T], pss[:, :T], AF.Sqrt, [kps], [f'rstd{ti % 2}'], bias=EPS, scale=1.0 / D)
            P.recip(r[:, :T], r[:, :T], [f'rstd{ti % 2}'], [f'rstd{ti % 2}'])

        load_x(0)
        stats(0)
        for ti, (t0, T, isctx) in enumerate(tiles):
            j = 1 if isctx else 0
            kx = 'xs0'
            xs = xsb[0].rearrange("p (c t) -> p c t", t=512)[:, :, :T]
            h3 = hbuf.rearrange("p (c t) -> p c t", t=512)
            rstd = rstdb[ti % 2]
            krs = f'rstd{ti % 2}'
            for c in range(KC):
                tm, ktm = tmp_next()
                P.stt('dve', tm[:, :T], xs[:, c, :], A_m[:, c, j:j + 1], rstd[:, :T], ALU.mult, ALU.mult,
                      [kx, 'der', krs], [ktm])
                P.act(h3[:, c, :T], tm[:, :T], AF.Identity, [ktm, 'mod'], [('h', c)], bias=B_m[:, c, j:j + 1],
                      scale=1.0)
            if ti + 1 < len(tiles):
                load_x(ti + 1)
            xb3 = xbcst.rearrange("p (c t) -> p c t", t=512)
            q3 = qst.rearrange("p (c t) -> p c t", t=512)
            cst = cs[ti % 2].rearrange("p (a t) -> p a t", t=512)
            for b in range(6):
                w, kw = pf.get()
                w3 = w.rearrange("p (k o) -> p k o", o=512)
                for oc in range(4):
                    g = b * 4 + oc
                    if g >= 22:
                        break
                    pso, kp = P.ps()
                    for kc in range(KC):
                        P.mm(pso[:, :T], w3[:, kc, oc * 128:(oc + 1) * 128], h3[:, kc, :T], kc == 0, kc == KC - 1,
                             [kw, ('h', kc)], [kp])
                    if g < 12:
                        if g % 2 == 0:
                            P.cp('act', xb3[:, g, :T], pso[:, :T], [kp], [('xbcst', g)])
                        else:
                            P.cp('dve', xb3[:, g, :T], pso[:, :T], [kp], [('xbcst', g)])
                    else:
                        hi = g - 12
                        if isctx:
                            P.cp('act', q3[:, hi, :T], pso[:, :T], [kp], [('qst', hi)])
                        else:
                            qq = qf[hi % 2]
                            kq = f'qf{hi % 2}'
                            tt1 = t1[hi % 2]
                            kt1 = f't1{hi % 2}'
                            P.cp('act', qq[:, :T], pso[:, :T], [kp], [kq])
                            ps2, kp2 = P.ps()
                            P.mm(ps2[:, :T], rot_f, qq[:, :T], True, True, [kq, 'cmat'], [kp2])
                            P.tt('pool', tt1[:, :T], qq[:, :T], cst[:, 0, :T], ALU.mult, [kq, f'cs{ti % 2}'], [kt1])
                            tm, ktm = tmp_next()
                            P.tt('dve', tm[:, :T], ps2[:, :T], cst[:, 1, :T], ALU.mult, [kp2, f'cs{ti % 2}'], [ktm])
                            P.tt('dve', q3[:, hi, :T], tm[:, :T], tt1[:, :T], ALU.add, [ktm, kt1], [('qst', hi)])
                if b == 1 and ti + 1 < len(tiles):
                    stats(ti + 1)
                if b == 2:
                    P.dma('sp', xbcT[:, t0:t0 + T].rearrange("(c p) t -> p c t", p=128), xb3[:, :, :T],
                          [('xbcst', None)], [('xbcT', ti)], 'st_xbc')
            P.dma('sp', qT_d[:, t0:t0 + T].rearrange("(c p) t -> p c t", p=128), q3[:, 0:8, :T],
                  [('qst', None)], [('qT', ti)], 'st_q')
            P.dma('sp', kT_d[:, t0:t0 + T].rearrange("(c p) t -> p c t", p=128), q3[:, 8:10, :T],
                  [('qst', None)], [('kT', ti)], 'st_q')
            nsub = T // 128
            zst = sqz.rearrange("p (s f) -> p s f", f=1024)
            vs3 = vst.rearrange("p (s f) -> p s f", f=256)
            ds3 = dst.rearrange("p (s f) -> p s f", f=32)
            for b in range(3):
                w, kw = pf.get()
                w3 = w.rearrange("p (k o) -> p k o", o=512)
                ncol = 512 if b < 2 else 288
                for sub in range(nsub):
                    pso, kp = P.ps()
                    for kc in range(KC):
                        P.mm(pso[:, :ncol], h3[:, kc, sub * 128:(sub + 1) * 128], w3[:, kc, 0:ncol], kc == 0,
                             kc == KC - 1, [kw, ('h', kc)], [kp])
                    if b < 2:
                        P.act(zst[:, sub, b * 512:(b + 1) * 512], pso[:, :], AF.Silu, [kp], [('sqz', ('z', sub, b))])
                    else:
                        P.cp('dve', vs3[:, sub, :], pso[:, 0:256], [kp], ['vst'])
                        P.tt('dve', dtt, pso[:, 256:288], vecB[:, 32:64], ALU.add, [kp, 'vecB'], ['dtt'])
                        P.act(dtt, dtt, AF.Exp, ['dtt'], ['dtt'])
                        P.act(ds3[:, sub, :], dtt, AF.Ln, ['dtt'], ['dst'], bias=1.0, scale=1.0)
            tsl = slice(t0, t0 + T)
            P.dma('sp', zs_d[tsl, :].rearrange("(s p) f -> p s f", p=128), zst[:, :nsub, :], [('sqz', None)],
                  [('zs', ti)], 'st_z')
            P.dma('sp', vtok_d[tsl, :].rearrange("(s p) f -> p s f", p=128), vs3[:, :nsub, :], ['vst'],
                  [('vtok', ti)], 'st_z')
            P.dma('sp', dts_d[tsl, :].rearrange("(s p) f -> p s f", p=128), ds3[:, :nsub, :], ['dst'],
                  [('dts', ti)], 'st_z')

    def phase_conv(l):
        P.bar()
        P.aoff = persist_end
        P.ps_n = 8
        ub = [P.alloc(12 * 516), P.alloc(12 * 516)]
        acc = [P.alloc(512) for _ in range(4)]
        xcs = P.alloc_bf(12 * 512)
        xtk = P.alloc_bf(4 * 1280)
        cw = vecF[:, 64:124].rearrange("p (c j) -> p c j", j=5)
        cb = vecF[:, 124:136]
        nt = len(tiles)

        def load_u(ti):
            t0, T, isctx = tiles[ti]
            u3 = ub[ti % 2].rearrange("p (c t) -> p c t", t=516)
            first = isctx or ti == 1
            last = isctx or ti == nt - 1
            lo = 0 if first else 2
            hi = 0 if last else 2
            ku = f'u{ti % 2}'
            if first:
                P.memset('pool', u3[:, :, 0:2], 0.0, [(ku, 'l')])
            if last:
                P.memset('pool', u3[:, :, T + 2:T + 4], 0.0, [(ku, 'r')])
            P.dma('sp', u3[:, :, 2 - lo:T + 2 + hi],
                  xbcT[:, t0 - lo:t0 + T + hi].rearrange("(c p) t -> p c t", p=128),
                  [('xbcT', None)], [(ku, 'm')], f'ld_u{ti % 2}')

        load_u(0)
        ai = 0
        for ti, (t0, T, isctx) in enumerate(tiles):
            if ti + 1 < nt:
                load_u(ti + 1)
            ku = f'u{ti % 2}'
            u3 = ub[ti % 2].rearrange("p (c t) -> p c t", t=516)
            x3 = xcs.rearrange("p (c t) -> p c t", t=512)
            for c in range(12):
                eng = 'dve'
                a = acc[ai % 4]
                ka = f'acc{ai % 4}'
                ai += 1
                P.ts(eng, a[:, :T], u3[:, c, 0:T], cw[:, c, 0:1], None, ALU.mult, None, [(ku, None), 'vecF'], [ka])
                for jj in range(1, 5):
                    P.stt(eng, a[:, :T], u3[:, c, jj:jj + T], cw[:, c, jj:jj + 1], a[:, :T], ALU.mult, ALU.add,
                          [(ku, None), 'vecF', ka], [ka])
                P.act(x3[:, c, :T], a[:, :T], AF.Silu, [ka, 'vecF'], [('xcs', c)], bias=cb[:, c:c + 1], scale=1.0)
            P.dma('sp', xcT_d[:, t0:t0 + T].rearrange("(c p) t -> p c t", p=128), x3[:, :, :T], [('xcs', None)],
                  [('xcT', ti)], 'st_xc')
            nsub = T // 128
            xt3 = xtk.rearrange("p (s f) -> p s f", f=1280)
            for sub in range(nsub):
                pA, kA = P.ps()
                pB, kB = P.ps()
                pAb = pA.bitcast(BF16)
                pBb = pB.bitcast(BF16)
                for c in range(8):
                    P.tr(pAb[:, c * 128:(c + 1) * 128], x3[:, c, sub * 128:(sub + 1) * 128], ident_b,
                         [('xcs', c), 'cbf'], [kA])
                for c in range(2):
                    P.tr(pBb[:, c * 128:(c + 1) * 128], x3[:, 8 + c, sub * 128:(sub + 1) * 128], ident_b,
                         [('xcs', 8 + c), 'cbf'], [kB])
                P.cp('dve', xt3[:, sub, 0:1024], pAb[:, 0:1024], [kA], [('xtk', sub)])
                P.cp('act', xt3[:, sub, 1024:1280], pBb[:, 0:256], [kB], [('xtk', sub)])
            P.dma('sp', xtok_d[t0:t0 + T, :].rearrange("(s p) f -> p s f", p=128), xt3[:, :nsub, :],
                  [('xtk', None)], [('xtok', ti)], 'st_xc')

    def phase_ssd(l, d):
        P.bar()
        P.aoff = persist_end
        NBUF = 2
        xtkb = [P.alloc_bf(1280) for _ in range(NBUF)]
        bctb = [P.alloc_bf(512) for _ in range(NBUF)]
        dtb = [P.alloc(32) for _ in range(NBUF)]
        zsb = [P.alloc(1024) for _ in range(NBUF)]
        yfb = [P.alloc(1024) for _ in range(NBUF)]
        hs = P.alloc(1024)
        hsb = P.alloc_bf(1024)
        ld = P.alloc(16)
        negacs = P.alloc(16)
        eacs = P.alloc(16)
        dtea = P.alloc(16)
        dte = P.alloc(16)
        cd = P.alloc(16)
        dec = [P.alloc(128) for _ in range(4)]
        MT = [P.alloc_bf(128) for _ in range(4)]
        xd = P.alloc_bf(1024)
        xdd = P.alloc_bf(1024)
        yacc = P.alloc(1024)
        ytmp = P.alloc(1024)
        ybf = P.alloc_bf(1024)
        yTs = P.alloc_bf(1024)
        ssq = P.alloc(2)
        junk = P.alloc(1024)
        P.memset('dve', hs, 0.0, ['hs'])
        P.memset('dve', hsb, 0.0, ['hsb'])
        if d == 0:
            order = [0, 1] + [2 + i for i in range(nlc)]
        else:
            order = [1, 0] + [2 + i for i in reversed(range(nlc))]
        llast = 127 if d == 0 else 0

        def load(i):
            cb_ = order[i]
            tc = cb_ * 128
            s = i % NBUF
            P.dma('sp', xtkb[s], xtok_d[tc:tc + 128, :], [('xtok', None)], [f'xtk{s}'], f'ld_a{s}')
            P.dma('sp', bctb[s].rearrange("p (c t) -> p c t", t=128),
                  xcT_d[1024:1536, tc:tc + 128].rearrange("(c p) t -> p c t", p=128), [('xcT', None)],
                  [f'bct{s}'], f'ld_a{s}')
            P.dma('sp', dtb[s], dts_d[tc:tc + 128, :], [('dts', None)], [f'dt{s}'], f'ld_a{s}')
            if d == 1:
                P.dma('sp', zsb[s], zs_d[tc:tc + 128, :], [('zs', None)], [f'zs{s}'], f'ld_b{s}')
                P.dma('sp', yfb[s], yf_d[tc:tc + 128, :], [('yf', None)], [f'yf{s}'], f'ld_b{s}')

        load(0)
        di = 0
        for i, cb_ in enumerate(order):
            if i + 1 < len(order):
                load(i + 1)
            s = i % NBUF
            tc = cb_ * 128
            xt = xtkb[s]
            kxt = f'xtk{s}'
            bct = bctb[s].rearrange("p (c t) -> p c t", t=128)
            kbct = f'bct{s}'
            dt_d = dtb[s][:, 16 * d:16 * d + 16]
            kdt = f'dt{s}'
            P.tt('dve', ld, dt_d, a_bc[:, 16 * d:16 * d + 16], ALU.mult, [kdt, 'a_bc'], ['ld'])
            ps0, k0 = P.bank(0)
            psm = ps0[:, 256:272]
            P.mm(psm, tri[d], ld, True, True, ['cmat', 'ld'], [k0])
            for g in range(2):
                P.mm(ps0[:, g * 128:(g + 1) * 128], bct[:, g, :], bct[:, 2 + g, :], True, True, [kbct], [k0])
            P.ts('dve', negacs, psm, -1.0, None, ALU.mult, None, [k0], ['negacs'])
            P.act(eacs, psm, AF.Exp, [k0], ['eacs'])
            P.tt('dve', xd.rearrange("p (h q) -> p h q", q=64), xt[:, 0:1024].rearrange("p (h q) -> p h q", q=64),
                 dt_d.unsqueeze(2).to_broadcast([128, 16, 64]), ALU.mult, [kxt, kdt], ['xd'])
            for g in range(2):
                po, ko = P.bank(5 + g)
                P.mm(po, bct[:, 2 + g, :], hsb[:, g * 512:(g + 1) * 512], True, True, [kbct, 'hsb'], [ko])
                sl = slice(g * 512, (g + 1) * 512)
                P.tt('dve', ytmp[:, sl].rearrange("p (h q) -> p h q", q=64), po.rearrange("p (h q) -> p h q", q=64),
                     eacs[:, g * 8:(g + 1) * 8].unsqueeze(2).to_broadcast([128, 8, 64]), ALU.mult, [ko, 'eacs'],
                     [('ytmp', g)])
            for r in range(2):
                pas = []
                for q in range(2):
                    pa, ka = P.bank(1 + q)
                    pas.append((pa, ka))
                    for hh in range(4):
                        h = r * 8 + q * 4 + hh
                        P.mm(pa[:, hh * 128:(hh + 1) * 128], ld[:, h:h + 1].to_broadcast([128, 128]), tri[d], True,
                             False, ['ld', 'cmat'], [ka])
                        P.mm(pa[:, hh * 128:(hh + 1) * 128], ident_f, mbias[d], False, True, ['cmat'], [ka])
                for q in range(2):
                    pa, ka = pas[q]
                    h0 = r * 8 + q * 4
                    lastcol = pa.rearrange("p (h t) -> p h t", t=128)[:, :, llast]
                    P.tt('dve', dtea[:, h0:h0 + 4], lastcol, negacs[:, h0:h0 + 4], ALU.add, [ka, 'negacs'],
                         [('dtea', h0)])
                    P.act(cd[:, h0:h0 + 4], lastcol, AF.Exp, [ka], [('cd', h0)])
                py, ky = P.bank(3 + r)
                for q in range(2):
                    pa, ka = pas[q]
                    for hh in range(4):
                        h = r * 8 + q * 4 + hh
                        dc = dec[di % 4]
                        kdc = f'dec{di % 4}'
                        m_ = MT[di % 4]
                        kmt = f'MT{di % 4}'
                        di += 1
                        P.act(dc, pa[:, hh * 128:(hh + 1) * 128], AF.Exp, [ka, 'negacs'], [kdc],
                              bias=negacs[:, h:h + 1], scale=1.0)
                        P.tt('dve', m_, ps0[:, r * 128:(r + 1) * 128], dc, ALU.mult, [k0, kdc], [kmt])
                        h8 = h % 8
                        P.mm(py[:, h8 * 64:(h8 + 1) * 64], m_, xd[:, h * 64:(h + 1) * 64], True, True, [kmt, 'xd'],
                             [ky])
                sl = slice(r * 512, (r + 1) * 512)
                P.tt('dve', yacc[:, sl], ytmp[:, sl], py, ALU.add, [('ytmp', r), ky], [('yacc', r)])
            P.act(dte, dtea, AF.Exp, [('dtea', None)], ['dte'])
            P.tt('dve', xdd.rearrange("p (h q) -> p h q", q=64), xd.rearrange("p (h q) -> p h q", q=64),
                 dte.unsqueeze(2).to_broadcast([128, 16, 64]), ALU.mult, ['xd', 'dte'], ['xdd'])
            P.tt('dve', hs.rearrange("p (h q) -> p h q", q=64), hs.rearrange("p (h q) -> p h q", q=64),
                 cd.unsqueeze(2).to_broadcast([128, 16, 64]), ALU.mult, ['hs', ('cd', None)], ['hs'])
            for g in range(2):
                pS, kS = P.bank(5 + g)
                P.mm(pS, xt[:, 1024 + g * 128:1024 + (g + 1) * 128], xdd[:, g * 512:(g + 1) * 512], True, True,
                     [kxt, 'xdd'], [kS])
                P.tt('dve', hs[:, g * 512:(g + 1) * 512], hs[:, g * 512:(g + 1) * 512], pS, ALU.add, ['hs', kS],
                     ['hs'])
            P.cp('act', hsb, hs, ['hs'], ['hsb'])
            if d == 0:
                P.dma('sp', yf_d[tc:tc + 128, :], yacc, [('yacc', None)], [('yf', cb_)], 'st_yf')
            else:
                P.tt('dve', yacc, yacc, yfb[s], ALU.add, [('yacc', None), f'yf{s}'], [('yacc', None)])
                P.tt('pool', ytmp.rearrange("p (h q) -> p h q", q=64), xt[:, 0:1024].rearrange("p (h q) -> p h q", q=64),
                     vecB[:, 64:80].unsqueeze(2).to_broadcast([128, 16, 64]), ALU.mult, [kxt, 'vecB'],
                     [('ytmp', None)])
                P.tt('dve', yacc, yacc, ytmp, ALU.add, [('yacc', None), ('ytmp', None)], [('yacc', None)])
                P.tt('dve', yacc, yacc, zsb[s], ALU.mult, [('yacc', None), f'zs{s}'], [('yacc', None)])
                P.act(junk, yacc, AF.Square, [('yacc', None)], ['junk', 'ssq'], accum_out=ssq[:, 0:1])
                P.act(ssq[:, 1:2], ssq[:, 0:1], AF.Sqrt, ['ssq'], ['ssq2'], bias=EPS, scale=1.0 / D_SSM)
                P.recip(ssq[:, 1:2], ssq[:, 1:2], ['ssq2'], ['ssq2'])
                P.stt('dve', ybf, yacc, ssq[:, 1:2], vecB[:, 80:1104], ALU.mult, ALU.mult,
                      [('yacc', None), 'ssq2', 'vecB'], ['ybf'])
                pT, kT_ = P.bank(7)
                pTb = pT.bitcast(BF16)
                for c in range(8):
                    P.tr(pTb[:, c * 128:(c + 1) * 128], ybf[:, c * 128:(c + 1) * 128], ident_b, ['ybf', 'cbf'], [kT_])
                P.cp('act', yTs, pTb[:, 0:1024], [kT_], ['yTs'])
                P.dma('sp', yT_d[0:1024, tc:tc + 128].rearrange("(c p) t -> p c t", p=128),
                      yTs.rearrange("p (c t) -> p c t", t=128), ['yTs'], [('yT', ('s', cb_))], 'st_yT')

    def phase_attn(l, last):
        P.bar()
        P.aoff = persist_end
        P.ps_n = 8
        kTg = P.alloc_bf(NT)
        Vg = P.alloc_bf(NT)
        q4b = [P.alloc_bf(2048), P.alloc_bf(2048)]
        pTb_ = [P.alloc_bf(512) for _ in range(6)]
        ost = [P.alloc_bf(2048), P.alloc_bf(2048)]
        den = [P.alloc(512), P.alloc(512)]
        scale = 1.0 / np.sqrt(128.0)
        qblocks = ([] if last else [0, 1]) + [2 + i for i in range(nlc)]
        groups = []
        if not last:
            groups.append([0, 1])
        for i in range(0, nlc, 4):
            groups.append([2 + i + k for k in range(4)])
        pi_box = [0]
        for g in range(2):
            P.dma('sp', kTg, kT_d[g * 128:(g + 1) * 128, :], [('kT', None)], ['kTg'], 'ld_kv')
            V3 = Vg.rearrange("p (b d) -> p b d", d=128)
            P.dma('sp', V3, vtok_d[:, g * 128:(g + 1) * 128].rearrange("(b p) d -> p b d", p=128),
                  [('vtok', None)], ['Vg'], 'ld_kv')
            for gi, grp in enumerate(groups):
                s = gi % 2
                nq = len(grp)
                tq0 = grp[0] * 128
                q4 = q4b[s].rearrange("p (h t) -> p h t", t=512)
                P.dma('sp', q4[:, :, :nq * 128],
                      qT_d[g * 512:(g + 1) * 512, tq0:tq0 + nq * 128].rearrange("(h p) t -> p h t", p=128),
                      [('qT', None)], [f'q4{s}'], f'ld_q{s}')
                o4 = ost[s].rearrange("p (h t) -> p h t", t=512)
                for qi, qb in enumerate(grp):
                    if qb < 2:
                        keys = [(0, None), (1, None)]
                    else:
                        n = qb - 2
                        keys = []
                        if n > 0:
                            keys.append((qb - 1, 0))
                        keys.append((qb, None))
                        if n < nlc - 1:
                            keys.append((qb + 1, 1))
                        keys += [(0, None), (1, None)]
                    rq = q4[:, :, qi * 128:(qi + 1) * 128]
                    psO_, kO = P.ps()
                    psD, kD = P.ps()
                    def do_S(ki):
                        kb, mk = keys[ki]
                        psS_, kS = P.ps()
                        P.mm(psS_.rearrange("p (h t) -> p h t", t=128), kTg[:, kb * 128:(kb + 1) * 128], rq, True,
                             mk is None, ['kTg', f'q4{s}'], [kS])
                        if mk is not None:
                            P.mm(psS_, ident_b, amask[:, mk * 512:(mk + 1) * 512], False, True, ['cbf', 'amask'], [kS])
                        i_ = pi_box[0] % 6
                        pi_box[0] += 1
                        pt = pTb_[i_]
                        kpt = f'pT{i_}'
                        P.act(pt, psS_, AF.Exp, [kS], [kpt], scale=float(scale))
                        return pt, kpt

                    cur = do_S(0)
                    for ki, (kb, mk) in enumerate(keys):
                        nx = do_S(ki + 1) if ki + 1 < len(keys) else None
                        pt, kpt = cur
                        P.mm(psO_, V3[:, kb, :], pt, ki == 0, ki == len(keys) - 1, ['Vg', kpt], [kO])
                        P.mm(psD, ones_b, pt, ki == 0, ki == len(keys) - 1, ['cbf', kpt], [kD])
                        cur = nx
                    dn = den[qi % 2]
                    kdn = f'den{qi % 2}'
                    P.tt('dve', dn, psD, expsink[:, g * 512:(g + 1) * 512], ALU.add, [kD, 'expsink'], [kdn])
                    P.recip(dn, dn, [kdn], [kdn])
                    P.tt('dve', o4[:, :, qi * 128:(qi + 1) * 128], psO_.rearrange("p (h t) -> p h t", t=128),
                         dn.rearrange("p (h t) -> p h t", t=128), ALU.mult, [kO, kdn], [(f'ost{s}', qi)])
                P.dma('sp', yT_d[1024 + g * 512:1024 + (g + 1) * 512, tq0:tq0 + nq * 128].rearrange(
                    "(h p) t -> p h t", p=128), o4[:, :, :nq * 128], [(f'ost{s}', None)], [('yT', ('a', g, gi))],
                    'st_yT')

    def phase_mlp(l, last):
        P.bar()
        P.aoff = persist_end
        P.ps_n = 7
        xb = P.alloc(8192)
        mixs = P.alloc(8192)
        bufA = P.alloc_bf(8192)
        ub = P.alloc_bf(32 * 512)
        ycat = P.alloc_bf(8192)
        rstd = P.alloc(512)
        tmp = [P.alloc(512) for _ in range(3)]
        rl = [P.alloc(512) for _ in range(2)]
        xsrc = xT0 if l == 0 else xTs
        my_tiles = [t for t in tiles if not (last and t[2])]
        seq = []
        for _ in my_tiles:
            seq += [blk(l, OFF_WOUT, i) for i in range(4)]
            for hf in range(2):
                seq += [blk(l, OFF_FF1, hf * 8 + i) for i in range(8)]
                seq += [blk(l, OFF_FF2, hf * 8 + i) for i in range(8)]
        pf = Prefetch(seq)
        cnt = {'tmp': 0, 'rl': 0}

        def nxt(lst, nm):
            i = cnt[nm] % len(lst)
            cnt[nm] += 1
            return lst[i], f'{nm}{i}'

        m3 = mixs.rearrange("p (c t) -> p c t", t=512)
        a3 = bufA.rearrange("p (c t) -> p c t", t=512)
        y3 = ycat.rearrange("p (c t) -> p c t", t=512)
        u3 = ub.rearrange("p (c t) -> p c t", t=512)

        def load_y(i):
            t0, T, isctx = my_tiles[i]
            P.dma('sp', y3[:, :, :T], yT_d[:, t0:t0 + T].rearrange("(c p) t -> p c t", p=128), [('yT', None)],
                  ['ycat'], 'ld_y')

        def finish_rstd(kps, pss, T):
            P.act(rstd[:, :T], pss[:, :T], AF.Sqrt, [kps], ['rstd'], bias=EPS, scale=1.0 / D)
            P.recip(rstd[:, :T], rstd[:, :T], ['rstd'], ['rstd'])

        load_y(0)
        for i_t, (t0, T, isctx) in enumerate(my_tiles):
            j = 1 if isctx else 0
            x3 = xb.rearrange("p (c t) -> p c t", t=512)[:, :, :T]
            P.dma('sp', x3, xsrc[:, t0:t0 + T].rearrange("(c p) t -> p c t", p=128),
                  [('xT', None)] if l > 0 else [], ['xb'], 'ld_xb')
            pss, kps = P.bank(7)
            pend = []

            def flush(last_=False):
                while pend:
                    c_ = pend.pop(0)
                    P.mm(pss[:, :T], ones_b, a3[:, c_, :T], c_ == 0, c_ == KC - 1, [('bufA', c_), 'cbf'], [kps])

            for b in range(4):
                w, kw = pf.get()
                w3 = w.rearrange("p (k o) -> p k o", o=512)
                for oc in range(4):
                    o = b * 4 + oc
                    pso, kp = P.ps()
                    for kc in range(KC):
                        P.mm(pso[:, :T], w3[:, kc, oc * 128:(oc + 1) * 128], y3[:, kc, :T], kc == 0, kc == KC - 1,
                             [kw, 'ycat'], [kp])
                    flush()
                    if o % 2 == 0:
                        P.cp('act', m3[:, o, :T], pso[:, :T], [kp], [('mixs', o)])
                    else:
                        P.cp('dve', m3[:, o, :T], pso[:, :T], [kp], [('mixs', o)])
                    P.tt('pool', a3[:, o, :T], m3[:, o, :T], m3[:, o, :T], ALU.mult, [('mixs', o)], [('bufA', o)])
                    pend.append(o)
            flush()
            finish_rstd(kps, pss, T)
            if i_t + 1 < len(my_tiles):
                load_y(i_t + 1)
            for c in range(KC):
                tm, ktm = nxt(tmp, 'tmp')
                P.tt('dve', tm[:, :T], m3[:, c, :T], rstd[:, :T], ALU.mult, [('mixs', c), 'rstd'], [ktm])
                P.stt('dve', x3[:, c, :], tm[:, :T], G_m[:, c, j:j + 1], x3[:, c, :], ALU.mult, ALU.add,
                      [ktm, 'der', ('xb', c)], [('xb', c)])
                P.tt('pool', a3[:, c, :T], x3[:, c, :], x3[:, c, :], ALU.mult, [('xb', c)], [('bufA', c)])
                P.mm(pss[:, :T], ones_b, a3[:, c, :T], c == 0, c == KC - 1, [('bufA', c), 'cbf'], [kps])
            finish_rstd(kps, pss, T)
            for c in range(KC):
                tm, ktm = nxt(tmp, 'tmp')
                P.stt('dve', tm[:, :T], x3[:, c, :], A_f[:, c, j:j + 1], rstd[:, :T], ALU.mult, ALU.mult,
                      [('xb', c), 'der', 'rstd'], [ktm])
                P.act(a3[:, c, :T], tm[:, :T], AF.Identity, [ktm, 'mod'], [('bufA', c)], bias=B_f[:, c, j:j + 1],
                      scale=1.0)
            for hf in range(2):
                for b in range(8):
                    w, kw = pf.get()
                    w3 = w.rearrange("p (k o) -> p k o", o=512)
                    for oc in range(4):
                        fcl = b * 4 + oc
                        pso, kp = P.ps()
                        for kc in range(KC):
                            P.mm(pso[:, :T], w3[:, kc, oc * 128:(oc + 1) * 128], a3[:, kc, :T], kc == 0, kc == KC - 1,
                                 [kw, ('bufA', kc)], [kp])
                        r, kr = nxt(rl, 'rl')
                        P.act(r[:, :T], pso[:, :T], AF.Relu, [kp], [kr])
                        P.tt('pool', u3[:, fcl, :T], r[:, :T], r[:, :T], ALU.mult, [kr], [('ub', fcl)])
                for b in range(8):
                    w, kw = pf.get()
                    w3 = w.rearrange("p (k o) -> p k o", o=256)
                    for oc in range(2):
                        o = b * 2 + oc
                        pso, kp = P.ps()
                        for fc in range(32):
                            P.mm(pso[:, :T], w3[:, fc, oc * 128:(oc + 1) * 128], u3[:, fc, :T], fc == 0, fc == 31,
                                 [kw, ('ub', fc)], [kp])
                        if hf == 0:
                            P.cp('act', m3[:, o, :T], pso[:, :T], [kp], [('mixs', o)])
                        else:
                            flush()
                            P.tt('dve', m3[:, o, :T], m3[:, o, :T], pso[:, :T], ALU.add, [('mixs', o), kp],
                                 [('mixs', o)])
                            P.tt('pool', a3[:, o, :T], m3[:, o, :T], m3[:, o, :T], ALU.mult, [('mixs', o)], [('bufA', o)])
                            pend.append(o)
            flush()
            finish_rstd(kps, pss, T)
            for c in range(KC):
                tm, ktm = nxt(tmp, 'tmp')
                P.tt('dve', tm[:, :T], m3[:, c, :T], rstd[:, :T], ALU.mult, [('mixs', c), 'rstd'], [ktm])
                P.stt('dve', x3[:, c, :], tm[:, :T], G_f[:, c, j:j + 1], x3[:, c, :], ALU.mult, ALU.add,
                      [ktm, 'der', ('xb', c)], [('xb', c)])
            if last:
                P.dma('sp', outT[:, t0 - LC:t0 - LC + T].rearrange("(c p) t -> p c t", p=128), x3, [('xb', None)],
                      [('outT', t0)], 'st_x')
            else:
                P.dma('sp', xTs[:, t0:t0 + T].rearrange("(c p) t -> p c t", p=128), x3, [('xb', None)],
                      [('xT', t0)], 'st_x')

    def convert_rest():
        rest = [blk(0, o, 0) + i for o, n in ((OFF_WOUT, 4), (OFF_FF1, 16), (OFF_FF2, 16)) for i in range(n)]
        for l_ in range(1, NL):
            rest += [blk(l_, OFF_WINF, i) for i in range(NBLK_L - OFF_WINF)]
        convert(rest)

    phases = []
    for l in range(NL):
        last = (l == NL - 1)
        phases += [lambda l=l: layer_setup(l), lambda l=l: (phase_mod(l), convert_rest() if l == 0 else None),
                   lambda l=l: phase_inproj(l),
                   lambda l=l: phase_conv(l), lambda l=l: phase_ssd(l, 0), lambda l=l: phase_ssd(l, 1),
                   lambda l=l, last=last: phase_attn(l, last), lambda l=l, last=last: phase_mlp(l, last)]
    if stop_after is not None:
        phases = phases[:stop_after]
    for ph in phases:
        ph()
    P.bar()
    P.op('sp', lambda e: e.nop(), (), ())
    P.S.emit()
    return nc


def _pack_k2048(w):
    K, C = w.shape
    nb = C // 512
    return np.ascontiguousarray(w.reshape(16, 128, nb, 512).transpose(2, 1, 0, 3)).reshape(nb, 128, 8192)


def _fm(v, nch):
    return np.ascontiguousarray(v.reshape(nch, 128).T)


def pack_weights(inp, NL):
    wall = np.zeros((NL * NBLK_L, 128, 8192), np.float32)
    vecF = np.zeros((NL, 128, NVF), np.float32)
    vecB = np.zeros((NL, 1, NVB), np.float32)
    for l in range(NL):
        base = l * NBLK_L
        wall[base + OFF_WMOD:base + OFF_WMOD + 24] = _pack_k2048(inp['w_mod'][l])
        wi = inp['w_in'][l]
        wF = np.zeros((2048, 3072), np.float32)
        wF[:, 0:1536] = wi[:, 0:1536]
        wF[:, 1536:2560] = wi[:, 2592:3616]
        wF[:, 2560:2816] = wi[:, 3616:3872]
        wall[base + OFF_WINF:base + OFF_WINF + 6] = _pack_k2048(wF)
        wT = np.zeros((2048, 1536), np.float32)
        wT[:, 0:1024] = wi[:, 1536:2560]
        wT[:, 1024:1280] = wi[:, 3872:4128]
        wT[:, 1280:1312] = wi[:, 2560:2592]
        wall[base + OFF_WINT:base + OFF_WINT + 3] = _pack_k2048(wT)
        wall[base + OFF_WOUT:base + OFF_WOUT + 4] = _pack_k2048(inp['w_out'][l])
        wall[base + OFF_FF1:base + OFF_FF1 + 16] = _pack_k2048(inp['w_ff1'][l])
        w2 = inp['w_ff2'][l]
        wall[base + OFF_FF2:base + OFF_FF2 + 16] = np.ascontiguousarray(
            w2.reshape(2, 32, 128, 8, 256).transpose(0, 3, 2, 1, 4)).reshape(16, 128, 8192)
        vecF[l, :, 0:16] = _fm(inp['g_pre_mix'][l], 16)
        vecF[l, :, 16:32] = _fm(inp['g_post_mix'][l], 16)
        vecF[l, :, 32:48] = _fm(inp['g_pre_mlp'][l], 16)
        vecF[l, :, 48:64] = _fm(inp['g_post_mlp'][l], 16)
        cw = inp['conv_w'][l]
        vecF[l, :, 64:124] = np.ascontiguousarray(cw.T.reshape(12, 128, 5).transpose(1, 0, 2)).reshape(128, 60)
        vecF[l, :, 124:136] = _fm(inp['conv_b'][l], 12)
        vecF[l, :, 136:232] = _fm(inp['b_mod'][l], 96)
        vecB[l, 0, 0:32] = inp['a_log'][l].reshape(32)
        vecB[l, 0, 32:64] = inp['dt_bias'][l].reshape(32)
        vecB[l, 0, 64:80] = inp['d_skip'][l]
        vecB[l, 0, 80:1104] = inp['ssm_norm'][l]
        vecB[l, 0, 1104:1112] = inp['attn_sink'][l]
    return wall, vecF, vecB


def make_consts(L):
    cm = np.zeros((128, 7 * 128), np.float32)
    i = np.arange(128)
    cm[:, 0:128] = np.eye(128)
    cm[:, 128:256] = (i[:, None] <= i[None, :])
    cm[:, 256:384] = (i[:, None] >= i[None, :])
    cm[:, 384:512] = np.where(i[:, None] <= i[None, :], 0.0, -BIG)
    cm[:, 512:640] = np.where(i[:, None] >= i[None, :], 0.0, -BIG)
    partner = np.where((i // 32) % 2 == 0, i + 32, i - 32)
    rot = np.zeros((128, 128), np.float32)
    rot[partner, i] = 1.0
    cm[:, 640:768] = rot
    cm[:, 768:896] = 1.0
    am = np.zeros((128, 1024), np.float32)
    prev = np.where(i[:, None] >= i[None, :], 0.0, -BIG)
    nxt = np.where(i[:, None] <= i[None, :], 0.0, -BIG)
    am[:, 0:512] = np.tile(prev, (1, 4))
    am[:, 512:1024] = np.tile(nxt, (1, 4))
    nf = 32
    inv = (10000.0 ** (-np.arange(nf, dtype=np.float32) / nf)).astype(np.float32)
    t = np.arange(L)
    pos_row = (t // 64).astype(np.float32)
    pos_col = (t % 64).astype(np.float32)
    ang_r = pos_row[None, :] * inv[:, None]
    ang_c = pos_col[None, :] * inv[:, None]
    cos = np.concatenate([np.cos(ang_r), np.cos(ang_r), np.cos(ang_c), np.cos(ang_c)], 0)
    sin = np.concatenate([-np.sin(ang_r), np.sin(ang_r), -np.sin(ang_c), np.sin(ang_c)], 0)
    rope = np.stack([cos, sin]).astype(np.float32)
    return cm, am, rope


def make_in_map(inp, b, L, shared):
    wall, vecF, vecB, cm, am, rope = shared
    xT = np.ascontiguousarray(np.concatenate([inp['ctx'][b], inp['x'][b][:L]], 0).T)
    ccv = np.stack([_fm(inp['c'][b], 16), _fm(inp['c_ctx'], 16)], -1).reshape(128, 32)
    return {"xT0": xT, "cc": np.ascontiguousarray(ccv), "wall": wall, "vecF": vecF, "vecB": vecB, "cmat": cm,
            "amask": am, "rope": rope}


def kernel(**inputs):
    inp = {k: np.asarray(v, dtype=np.float32) for k, v in inputs.items()}
    L = inp['x'].shape[1]
    NL = inp['w_in'].shape[0]
    B = inp['x'].shape[0]
    nc = build_program(L, NL)
    shared = pack_weights(inp, NL) + make_consts(L)
    in_maps = [make_in_map(inp, c % B, L, shared) for c in range(8)]
    res = run_bass_kernel_spmd(nc, in_maps, core_ids=list(range(8)))
    out = np.stack([np.ascontiguousarray(res.results[b]["outT"].T) for b in range(B)], 0)
    return out.astype(np.float32)
```
